# Optimizing a Trainium2 kernel written in Bass

```python
import jax, jax.numpy as jnp
from jax import lax
import numpy as np

D_MODEL = 1024
BATCH = 8
SEQ = 4096
DEPTH = 1

CHUNK = 64
Q_BLOCK = 64
D_CONV = D_MODEL // 2
CONV_WIDTH = 31
N_HEADS = 8
HEAD_DIM = 64
D_ATTN = N_HEADS * HEAD_DIM
N_IDX_HEADS = 8
IDX_DIM = 64
TOPK_MAX = 256
D_FF = 4 * D_MODEL
N_BRANCHES = 2
ROPE_THETA = 10000.0
LN_EPS = 1e-5
NEG_INF = -1e30
DEEPNORM_ALPHA = (2.0 * DEPTH) ** 0.25
DEEPNORM_BETA = (8.0 * DEPTH) ** -0.25
IN_SPLIT_SIZES = (D_CONV, D_CONV, D_ATTN, D_ATTN, D_ATTN,
                  N_IDX_HEADS * IDX_DIM, IDX_DIM, N_IDX_HEADS, D_MODEL, D_MODEL)
D_IN = sum(IN_SPLIT_SIZES)

kernel_name = "hybrid_conformer_conv_dsa_gated_deepnorm"


def layer_norm(x, g, b):
    xf = x.astype(jnp.float32)
    mu = jnp.mean(xf, axis=-1, keepdims=True)
    var = jnp.mean(jnp.square(xf - mu), axis=-1, keepdims=True)
    return ((xf - mu) * lax.rsqrt(var + LN_EPS) * g.astype(jnp.float32)
            + b.astype(jnp.float32)).astype(x.dtype)


def rope(x, pos):
    d = x.shape[-1]
    inv_freq = ROPE_THETA ** (-jnp.arange(0, d, 2, dtype=jnp.float32) / d)
    ang = pos.astype(jnp.float32)[:, None] * inv_freq[None, :]
    cos = jnp.cos(ang)[:, None, :]
    sin = jnp.sin(ang)[:, None, :]
    xf = x.astype(jnp.float32)
    x1, x2 = xf[..., : d // 2], xf[..., d // 2:]
    return jnp.concatenate([x1 * cos - x2 * sin, x2 * cos + x1 * sin], axis=-1).astype(x.dtype)


def conformer_conv_branch(a, b, dw_w, dw_b, ln_g, ln_b, w_out):
    u = a * jax.nn.sigmoid(b)
    u = lax.conv_general_dilated(
        u, dw_w[:, None, :].astype(u.dtype), window_strides=(1,),
        padding=[(CONV_WIDTH - 1, 0)],
        dimension_numbers=('NWC', 'WIO', 'NWC'),
        feature_group_count=D_CONV) + dw_b
    u = jax.nn.silu(layer_norm(u, ln_g, ln_b))
    return u @ w_out


def dsa_sparse_attention(q, k, v, qi, ki, wi, pos):
    B, L = q.shape[0], q.shape[1]
    dt = q.dtype
    topk = min(TOPK_MAX, L // 4)
    n_blk = L // Q_BLOCK
    key_chunk = pos // CHUNK
    ki_f = ki.astype(jnp.float32)
    kv = jnp.concatenate([k, v], axis=-1)

    def to_blocks(t):
        return t.reshape(B, n_blk, Q_BLOCK, *t.shape[2:]).swapaxes(0, 1)

    def attend_block(args):
        qb, qib, wb, start = args
        q_chunk = (start + jnp.arange(Q_BLOCK)) // CHUNK
        admissible = key_chunk[None, :] <= q_chunk[:, None]
        logits = jnp.einsum('bqhd,bsd->bqhs', qib.astype(jnp.float32), ki_f) * (IDX_DIM ** -0.5)
        iscore = jnp.einsum('bqhs,bqh->bqs', jax.nn.relu(logits), wb.astype(jnp.float32))
        iscore = jnp.where(admissible[None], iscore, NEG_INF)
        _, idx = lax.top_k(iscore, topk)
        valid = (idx // CHUNK) <= q_chunk[None, :, None]
        kv_sel = jax.vmap(lambda kvb, ib: kvb[ib])(kv, idx)
        k_sel, v_sel = jnp.split(kv_sel, 2, axis=-1)
        s = jnp.einsum('bqhd,bqkhd->bhqk', qb.astype(jnp.float32),
                       k_sel.astype(jnp.float32)) * (HEAD_DIM ** -0.5)
        s = jnp.where(valid[:, None], s, NEG_INF)
        p = jax.nn.softmax(s, axis=-1).astype(dt)
        return jnp.einsum('bhqk,bqkhd->bqhd', p, v_sel)

    starts = jnp.arange(n_blk, dtype=jnp.int32) * Q_BLOCK
    o = lax.map(attend_block, (to_blocks(q), to_blocks(qi), to_blocks(wi), starts))
    return o.swapaxes(0, 1).reshape(B, L, N_HEADS * HEAD_DIM)


def setup_inputs(seed: int = 0) -> dict:
    key = jax.random.key(seed)
    ks = jax.random.split(key, 20)
    f32 = jnp.float32

    def nrm(k, shape, scale):
        return jax.random.normal(k, shape, f32) * scale

    def gain(k, n):
        return 1.0 + 0.05 * jax.random.normal(k, (DEPTH, n), f32)

    return {
        "x": jax.random.normal(ks[0], (BATCH, SEQ, D_MODEL), f32),
        "w_in": nrm(ks[1], (DEPTH, D_MODEL, D_IN), D_MODEL ** -0.5),
        "dw_w": nrm(ks[2], (DEPTH, CONV_WIDTH, D_CONV), CONV_WIDTH ** -0.5),
        "dw_b": nrm(ks[3], (DEPTH, D_CONV), 0.02),
        "conv_ln_g": gain(ks[4], D_CONV),
        "conv_ln_b": nrm(ks[5], (DEPTH, D_CONV), 0.02),
        "w_conv_out": nrm(ks[6], (DEPTH, D_CONV, D_MODEL), D_CONV ** -0.5),
        "idx_k_ln_g": gain(ks[7], IDX_DIM),
        "idx_k_ln_b": nrm(ks[8], (DEPTH, IDX_DIM), 0.02),
        "w_attn_out": nrm(ks[9], (DEPTH, D_ATTN, D_MODEL), D_ATTN ** -0.5),
        "gate_b": nrm(ks[10], (DEPTH, N_BRANCHES, D_MODEL), 0.02),
        "w_out": nrm(ks[11], (DEPTH, D_MODEL, D_MODEL), DEEPNORM_BETA * D_MODEL ** -0.5),
        "ln1_g": gain(ks[12], D_MODEL),
        "ln1_b": nrm(ks[13], (DEPTH, D_MODEL), 0.02),
        "w_ff_in": nrm(ks[14], (DEPTH, D_MODEL, D_FF), D_MODEL ** -0.5),
        "w_ff_out": nrm(ks[15], (DEPTH, D_FF, D_MODEL), DEEPNORM_BETA * D_FF ** -0.5),
        "ln2_g": gain(ks[16], D_MODEL),
        "ln2_b": nrm(ks[17], (DEPTH, D_MODEL), 0.02),
    }


def reference(x, w_in, dw_w, dw_b, conv_ln_g, conv_ln_b, w_conv_out, idx_k_ln_g, idx_k_ln_b,
              w_attn_out, gate_b, w_out, ln1_g, ln1_b, w_ff_in, w_ff_out, ln2_g, ln2_b):
    B, L, _ = x.shape
    pos = jnp.arange(L, dtype=jnp.int32)
    split_points = np.cumsum(np.array(IN_SPLIT_SIZES))[:-1].tolist()
    for layer in range(DEPTH):
        proj = x @ w_in[layer]
        (conv_a, conv_b, q, k, v, qi, ki, wi, g_conv, g_attn) = jnp.split(proj, split_points, axis=-1)

        conv_out = conformer_conv_branch(conv_a, conv_b, dw_w[layer], dw_b[layer],
                                         conv_ln_g[layer], conv_ln_b[layer], w_conv_out[layer])

        q = rope(q.reshape(B, L, N_HEADS, HEAD_DIM), pos)
        k = rope(k.reshape(B, L, N_HEADS, HEAD_DIM), pos)
        v = v.reshape(B, L, N_HEADS, HEAD_DIM)
        qi = rope(qi.reshape(B, L, N_IDX_HEADS, IDX_DIM), pos)
        ki = rope(layer_norm(ki, idx_k_ln_g[layer], idx_k_ln_b[layer])[:, :, None, :], pos)[:, :, 0, :]
        wi = wi * (N_IDX_HEADS ** -0.5)
        attn = dsa_sparse_attention(q, k, v, qi, ki, wi, pos)
        attn_out = attn @ w_attn_out[layer]

        merged = (jax.nn.sigmoid(g_conv + gate_b[layer, 0]) * conv_out
                  + jax.nn.sigmoid(g_attn + gate_b[layer, 1]) * attn_out)
        mixer = merged @ w_out[layer]
        x = layer_norm(DEEPNORM_ALPHA * x + mixer, ln1_g[layer], ln1_b[layer])

        h = jnp.square(jax.nn.relu(x @ w_ff_in[layer]))
        x = layer_norm(DEEPNORM_ALPHA * x + h @ w_ff_out[layer], ln2_g[layer], ln2_b[layer])
    return x
```

```python
import numpy as np
from contextlib import ExitStack
import concourse.bass as bass
import concourse.mybir as mybir
from concourse.bass_utils import run_bass_kernel_spmd

F32 = mybir.dt.float32
BF16 = mybir.dt.bfloat16
U8 = mybir.dt.uint8
AF = mybir.ActivationFunctionType
ALU = mybir.AluOpType
AX = mybir.AxisListType

D = 1024
NB_BISECT = 22
G = 256
NSUB = G // 128
CHW = 4096
NSLOT = 3
ALPHA = 2.0 ** 0.25
LN_EPS = 1e-5
NEG = -1.0e30
N_CHUNK = 40


class Eng:
    def __init__(self, name, sem, same_sync):
        self.name = name
        self.sem = sem
        self.count = 0
        self.waited = {}
        self.ops = []
        self.same_sync = same_sync


class Rec:
    def __init__(self):
        self.call = None

    def __getattr__(self, name):
        def f(*a, **k):
            self.call = (name, a, k)
            return self
        return f


class Dep:
    def __init__(self, name="dep"):
        self.name = name
        self.last_write = None
        self.reads = []
        self.dma_eng = None


class Tile(Dep):
    def __init__(self, name, t, excl=False):
        Dep.__init__(self, name)
        self.t = t
        self.excl = excl

    def __getitem__(self, idx):
        return self.t[idx]


class FW:
    def __init__(self, nc, stack):
        self.nc = nc
        self.stack = stack
        self.engs = {}
        self.muted = False
        for n in ["tensor", "vector", "scalar", "gpsimd", "sync"]:
            sem = stack.enter_context(nc.semaphore("sem_" + n))
            self.engs[n] = Eng(n, sem, same_sync=(n in ("vector", "scalar", "gpsimd")))

    def sb(self, name, shape, dt):
        return Tile(name, self.stack.enter_context(self.nc.sbuf_tensor("sb_" + name, shape, dt)))

    def ps(self, name, shape, dt):
        return Tile(name, self.stack.enter_context(self.nc.psum_tensor("ps_" + name, shape, dt)), excl=True)

    def dma_eng(self, name):
        sem = self.stack.enter_context(self.nc.semaphore("dsem_" + name))
        return Eng("dma_" + name, sem, False)

    def _deps(self, eng, reads, writes):
        deps = []
        for t in reads:
            if t.last_write is not None:
                deps.append(t.last_write)
        for t in writes:
            if t.last_write is not None:
                deps.append(t.last_write)
            deps.extend(t.reads)
        waits = {}
        for (e2, seq) in deps:
            if e2 is eng and not eng.same_sync:
                continue
            if eng.waited.get(e2, 0) >= seq:
                continue
            if waits.get(e2, 0) < seq:
                waits[e2] = seq
        for e2, seq in waits.items():
            eng.waited[e2] = seq
        return list(waits.items())

    def _mark(self, who, seq, reads, writes):
        for t in reads:
            t.reads.append((who, seq))
        for t in writes:
            t.last_write = (who, seq)
            t.reads = []

    def op(self, engname, fn, reads=(), writes=()):
        if self.muted:
            return
        eng = self.engs[engname]
        ex = [t for t in reads if getattr(t, "excl", False)]
        if ex:
            writes = list(writes) + [t for t in ex if t not in writes]
        waits = self._deps(eng, reads, writes)
        eng.count += 1
        seq = eng.count
        rec = Rec()
        fn(rec)
        mname, margs, mkw = rec.call

        def thunk(h, waits=waits, eng=eng, mname=mname, margs=margs, mkw=mkw):
            for e2, s in waits:
                h.wait_ge(e2.sem, s)
            getattr(h, mname)(*margs, **mkw).then_inc(eng.sem, 1)

        eng.ops.append(thunk)
        self._mark(eng, seq, reads, writes)

    def dma(self, qname, out_fn, in_fn, reads=(), writes=(), pe=None):
        if self.muted:
            return None
        q = self.engs[qname]
        waits = self._deps(q, reads, writes)
        if pe is None:
            key = writes[0] if writes else reads[0]
            if key.dma_eng is None:
                key.dma_eng = self.dma_eng(key.name)
            pe = key.dma_eng
        pe.count += 16
        seq = pe.count
        o_ap = out_fn()
        i_ap = in_fn()

        def thunk(h, waits=waits, pe=pe, o_ap=o_ap, i_ap=i_ap):
            for e2, s in waits:
                h.wait_ge(e2.sem, s)
            h.dma_start(out=o_ap, in_=i_ap).then_inc(pe.sem, 16)

        q.ops.append(thunk)
        self._mark(pe, seq, reads, writes)
        return pe, seq

    def finish(self, final_waits):
        nc = self.nc
        engs = self.engs
        with nc.Block() as block:
            def mk(name):
                eng = engs[name]

                def body(h):
                    for th in eng.ops:
                        th(h)
                    if name == "sync":
                        for pe, s in final_waits:
                            h.wait_ge(pe.sem, s)
                return body
            block.tensor(mk("tensor"))
            block.vector(mk("vector"))
            block.scalar(mk("scalar"))
            block.gpsimd(mk("gpsimd"))
            block.sync(mk("sync"))


def _fm_block(w, cols):
    K = w.shape[0]
    blk = w[:, cols]
    return blk.reshape(K // 128, 128, 128).transpose(1, 0, 2).reshape(128, -1)


def _kc_major(w):
    K, N = w.shape
    return w.reshape(K // 128, 128, N).transpose(1, 0, 2).reshape(128, -1)


def host_weights(inp):
    w_in = np.asarray(inp["w_in"][0], np.float32)
    oa, ob, oq, ok, ov, oqi, oki, owi, ogc, oga = 0, 512, 1024, 1536, 2048, 2560, 3072, 3136, 3144, 4168
    sw = np.arange(512).reshape(8, 64)
    sw = np.concatenate([sw[:, 32:], sw[:, :32]], axis=1).ravel()
    chunks = []

    def pad(a):
        out = np.zeros((128, CHW), np.float32)
        out[:, :a.shape[1]] = a
        return out

    def fm4(cols512):
        return np.concatenate([_fm_block(w_in, cols512[b * 128:(b + 1) * 128]) for b in range(4)], axis=1)

    r = np.arange(512)
    chunks.append(fm4(ob + r))
    chunks.append(fm4(oa + r))
    chunks.append(fm4(oq + r))
    chunks.append(fm4(oq + sw))
    chunks.append(fm4(ok + r))
    chunks.append(fm4(ok + sw))
    chunks.append(fm4(oqi + r))
    chunks.append(fm4(oqi + sw))
    chunks.append(_kc_major(w_in[:, ov:ov + 512]))
    chunks.append(pad(_kc_major(w_in[:, oki:oki + 72])))
    w_co = np.asarray(inp["w_conv_out"][0], np.float32)
    w_ao = np.asarray(inp["w_attn_out"][0], np.float32)
    r128 = np.arange(128)
    for j in range(8):
        blks = [_fm_block(w_in, ogc + j * 128 + r128), _fm_block(w_in, oga + j * 128 + r128),
                _fm_block(w_co, j * 128 + r128), _fm_block(w_ao, j * 128 + r128)]
        chunks.append(pad(np.concatenate(blks, axis=1)))
    w_out = np.asarray(inp["w_out"][0], np.float32)
    for h in range(2):
        chunks.append(_kc_major(w_out[:, h * 512:(h + 1) * 512]))
    w_fi = np.asarray(inp["w_ff_in"][0], np.float32)
    for c in range(8):
        chunks.append(np.concatenate([_fm_block(w_fi, (4 * c + b) * 128 + r128) for b in range(4)], axis=1))
    w_fo = np.asarray(inp["w_ff_out"][0], np.float32)
    for h in range(2):
        for c in range(4):
            chunks.append(_kc_major(w_fo[c * 1024:(c + 1) * 1024, h * 512:(h + 1) * 512]))
    dw_w = np.asarray(inp["dw_w"][0], np.float32)
    for j in range(4):
        dg = np.zeros((128, 31, 128), np.float32)
        idx = np.arange(128)
        dg[idx, :, idx] = dw_w[:, j * 128:(j + 1) * 128].T
        chunks.append(pad(dg.reshape(128, 31 * 128)))
    wall = np.stack(chunks, axis=0)
    assert wall.shape == (N_CHUNK, 128, CHW), wall.shape
    return np.ascontiguousarray(wall)


def host_consts(inp, L):
    nb = NB_BISECT
    cvec = np.zeros((128, 152 + nb), np.float32)
    dw_w = np.asarray(inp["dw_w"][0], np.float32)
    for j in range(4):
        cvec[:, j * 31:(j + 1) * 31] = dw_w[:, j * 128:(j + 1) * 128].T
    cvec[:, 124:128] = np.asarray(inp["dw_b"][0]).reshape(4, 128).T
    cvec[:, 128:132] = np.asarray(inp["conv_ln_g"][0]).reshape(4, 128).T
    cvec[:, 132:136] = np.asarray(inp["conv_ln_b"][0]).reshape(4, 128).T
    gb = np.asarray(inp["gate_b"][0], np.float32)
    cvec[:, 136:144] = gb[0].reshape(8, 128).T
    cvec[:, 144:152] = gb[1].reshape(8, 128).T
    cvec[:, 152:152 + nb] = (0.5 ** np.arange(1, nb + 1))[None, :]
    lnbc = np.stack([np.broadcast_to(np.asarray(inp[k][0], np.float32)[None, :], (128, D))
                     for k in ("ln1_g", "ln1_b", "ln2_g", "ln2_b")], axis=1)
    idxbc = np.stack([np.broadcast_to(np.asarray(inp[k][0], np.float32)[None, :], (128, 64))
                      for k in ("idx_k_ln_g", "idx_k_ln_b")], axis=1)
    inv_freq = (10000.0 ** (-np.arange(0, 64, 2, dtype=np.float32) / np.float32(64))).astype(np.float32)
    ang = (np.arange(L, dtype=np.float32)[:, None] * inv_freq[None, :]).astype(np.float32)
    cos = np.cos(ang.astype(np.float64)).astype(np.float32)
    sin = np.sin(ang.astype(np.float64)).astype(np.float32)
    cos2 = np.concatenate([cos, cos], axis=1)
    sinS = np.concatenate([-sin, sin], axis=1)
    ropeT = np.concatenate([cos2, sinS], axis=1)
    ropeF = np.stack([np.concatenate([cos2.T, cos2.T], axis=0),
                      np.concatenate([sinS.T, sinS.T], axis=0)], axis=1)
    identF = np.eye(128, dtype=np.float32)
    return dict(cvec=cvec, lnbc=np.ascontiguousarray(lnbc), idxbc=np.ascontiguousarray(idxbc),
                ropeT=np.ascontiguousarray(ropeT), ropeF=np.ascontiguousarray(ropeF), identF=identF)


class _Stop(Exception):
    pass


def build_program(L, topk):
    import os
    STOP = os.environ.get('K_STOP', '')
    NGRUN = int(os.environ.get('K_NG', '0'))
    STOPS = int(os.environ.get('K_STOPS', '0'))
    NG = L // G
    NT = L // 128
    nb = NB_BISECT
    NV = 152 + nb
    nc = bass.Bass("TRN2", target_bir_lowering=False)
    x_d = nc.dram_tensor("x", [L, D], F32, kind="ExternalInput").ap()
    wall_d = nc.dram_tensor("wall", [N_CHUNK, 128, CHW], F32, kind="ExternalInput").ap()
    cvec_d = nc.dram_tensor("cvec", [128, NV], F32, kind="ExternalInput").ap()
    lnbc_d = nc.dram_tensor("lnbc", [128, 4, D], F32, kind="ExternalInput").ap()
    idxbc_d = nc.dram_tensor("idxbc", [128, 2, 64], F32, kind="ExternalInput").ap()
    ropeT_d = nc.dram_tensor("ropeT", [L, 128], F32, kind="ExternalInput").ap()
    ropeF_d = nc.dram_tensor("ropeF", [128, 2, L], F32, kind="ExternalInput").ap()
    identF_d = nc.dram_tensor("identF", [128, 128], F32, kind="ExternalInput").ap()
    out_d = nc.dram_tensor("out", [L, D], F32, kind="ExternalOutput").ap()
    wscr_d = nc.dram_tensor("wscr", [N_CHUNK, 128, CHW], BF16).ap()

    with ExitStack() as st:
        fw = FW(nc, st)
        KT = fw.sb("KT", [128, 4, L], BF16)
        Vp = fw.sb("Vp", [128, NT, 8, 65], BF16)
        kiT = fw.sb("kiT", [128, L], BF16)
        KTd = [Dep("KT%d" % g) for g in range(NG)]
        Vpd = [Dep("Vp%d" % g) for g in range(NG)]
        kiTd = [Dep("kiT%d" % g) for g in range(NG)]
        xg = [fw.sb("xg%d" % i, [128, NSUB, D], F32) for i in range(2)]
        xT = fw.sb("xT", [128, 8, G], BF16)
        s4a = fw.sb("s4a", [128, 4, G], F32)
        u = fw.sb("u", [128, 4, 30 + G], BF16)
        acc = fw.sb("acc", [128, 4, G], F32)
        convact = fw.sb("convact", [128, 4, G], BF16)
        rtmp = fw.sb("rtmp", [128, G], F32)
        qT = fw.sb("qT", [128, 4, G], BF16)
        qiT = fw.sb("qiT", [128, 4, G], BF16)
        ropeFg = fw.sb("ropeFg", [128, 2, G], F32)
        ropeTg = fw.sb("ropeTg", [128, NSUB, 128], F32)
        iscore = fw.sb("iscore", [128, max(L, CHW)], F32)
        junk = fw.sb("junk", [128, L], U8)
        mask8 = fw.sb("mask8", [128, 1024], BF16)
        big16 = fw.sb("big16", [128, 8192], BF16)
        maskTd = [Dep("maskT0"), Dep("maskT1")]
        relu_ts = [fw.sb("relu_t%d" % i, [128, 512], F32) for i in range(2)]
        relu_t = relu_ts[0]
        PT = [fw.sb("PT%d" % i, [128, 512], BF16) for i in range(4)]
        attn = fw.sb("attn", [128, 512], BF16)
        attnT = fw.sb("attnT", [128, 4, G], BF16)
        mergedT = fw.sb("mergedT", [128, 8, G], BF16)
        lnbc = fw.sb("lnbc", [128, 2, D], F32)
        idxbc = fw.sb("idxbc", [128, 2, 64], F32)
        cvec = fw.sb("cvec", [128, NV], F32)
        identF = fw.sb("identF", [128, 128], F32)
        identB = fw.sb("identB", [128, 128], BF16)
        onesM = fw.sb("onesM", [128, 128], F32)
        kn = fw.sb("kn", [128, 64], F32)
        kA = fw.sb("kA", [128, 64], F32)
        kB = fw.sb("kB", [128, 64], F32)
        kr2 = fw.sb("kr2", [128, 128], F32)
        wsc = fw.sb("wsc", [128, NSUB, 8], F32)
        small = fw.sb("small", [128, 64], F32)
        steps = fw.sb("steps", [128, nb], F32)
        small_ln = fw.sb("small_ln", [128, 8], F32)
        t_mid = fw.sb("t_mid", [128, 1], F32)
        t_cd = fw.sb("t_cd", [128, 1], F32)
        t_sg = fw.sb("t_sg", [128, 1], F32)
        t_t = fw.sb("t_t", [128, 1], F32)
        t_d = fw.sb("t_d", [128, 1], F32)
        cvec2 = fw.sb("cvec2", [128, 2 * (L // 128) + 4], F32)
        _ht = {}

        def half_thr(v):
            if v not in _ht:
                col = len(_ht)
                fw.op("gpsimd", lambda h: h.memset(cvec2.t[:, col:col + 1], v * 0.5), writes=[cvec2])
                _ht[v] = col
            c_ = _ht[v]
            return cvec2.t[:, c_:c_ + 1]

        junk_lo = Dep("junk_lo")
        junk_hi = Dep("junk_hi")
        stats = fw.sb("stats", [128, 12], F32)
        rden = fw.sb("rden", [128, 8], F32)
        cmean = fw.sb("cmean", [128, G], F32)
        crstd = fw.sb("crstd", [128, G], F32)
        slots = [fw.sb("wslot%d" % i, [128, CHW], BF16) for i in range(NSLOT)]
        NPX = 5
        pX = [fw.ps("pX%d" % i, [128, 512], F32) for i in range(NPX)]
        pO = [fw.ps("pO%d" % i, [128, 512], F32) for i in range(2)]
        pTb = fw.ps("pTb", [128, 1024], BF16)
        pxi = [0]

        def nextpx():
            p = pX[pxi[0] % NPX]
            pxi[0] += 1
            return p

        SM = small.t
        c_mean, c_var, c_rstd, c_lo, c_mid, c_cnt, c_d, c_mx, c_mn, c_rng, c_thr = range(11)

        def sc(i):
            return SM[:, i:i + 1]

        def scl(i):
            return small_ln.t[:, i:i + 1]

        def cv(i):
            return cvec.t[:, i:i + 1]

        fw.dma("sync", lambda: cvec.t[:, :], lambda: cvec_d[:, :], writes=[cvec])
        fw.dma("sync", lambda: identF.t[:, :], lambda: identF_d[:, :], writes=[identF])
        fw.dma("sync", lambda: idxbc.t[:, :, :], lambda: idxbc_d[:, :, :], writes=[idxbc])
        fw.op("vector", lambda h: h.tensor_copy(identB.t[:, :], identF.t[:, :]), reads=[identF], writes=[identB])
        fw.op("gpsimd", lambda h: h.memset(onesM.t[:, :], 1.0 / 512.0), writes=[onesM])
        fw.op("gpsimd", lambda h: h.memset(u.t[:, :, :], 0.0), writes=[u])
        fw.op("gpsimd", lambda h: h.memset(Vp.t[:, :, :, :].rearrange("p a b c -> p (a b) c")[:, :, 64:65], 1.0), writes=Vpd)

        scr_chunk = [Dep("wscr%d" % i) for i in range(N_CHUNK)]
        conv_order = [6, 7, 9, 0, 1, 2, 3, 4, 5, 8] + list(range(36, 40)) + list(range(10, 36))
        assert sorted(conv_order) == list(range(N_CHUNK))
        for i in conv_order:
            fw.dma("gpsimd", lambda i=i: wscr_d[i, :, :], lambda i=i: wall_d[i, :, :], writes=[scr_chunk[i]],
                   pe=fw.dma_eng("wconv%d" % i))

        PA_ORDER = [6, 7, 9]
        PB_ORDER = [0, 1, 2, 3, 4, 5, 8]
        order = PA_ORDER + PB_ORDER + list(range(36, 40))
        for gg in range(NG):
            order += list(range(10, 20))
            if gg + 1 < NG:
                order += PA_ORDER
            order += list(range(20, 36))
            if gg + 1 < NG:
                order += PB_ORDER + list(range(36, 40))
        stream = {"issued": 0, "next": 0}
        total_chunks = len(order)

        def issue_upto(n):
            while stream["issued"] < min(n, total_chunks):
                k = stream["issued"]
                sl = slots[k % NSLOT]
                fw.dma("sync", lambda sl=sl: sl.t[:, :], lambda k=k: wscr_d[order[k], :, :], reads=[scr_chunk[order[k]]], writes=[sl])
                stream["issued"] += 1

        def next_chunk(expect):
            k = stream["next"]
            assert order[k] == expect, (k, order[k], expect)
            issue_upto(k + NSLOT)
            stream["next"] += 1
            return slots[k % NSLOT]

        out_waits = []
        out_pes = [fw.dma_eng("outst%d" % i) for i in range(2)]

        def load_x(g):
            xt = xg[g % 2]
            fw.dma("sync", lambda: xt.t[:, :, :],
                   lambda: x_d[g * G:(g + 1) * G, :].rearrange("(s p) d -> p s d", p=128), writes=[xt])

        def load_tabs(g):
            fw.dma("sync", lambda: ropeFg.t[:, :, :], lambda: ropeF_d[:, :, g * G:(g + 1) * G], writes=[ropeFg])
            fw.dma("sync", lambda: ropeTg.t[:, :, :],
                   lambda: ropeT_d[g * G:(g + 1) * G, :].rearrange("(s p) d -> p s d", p=128), writes=[ropeTg])

        def load_ln(which):
            fw.dma("sync", lambda: lnbc.t[:, :, :], lambda: lnbc_d[:, 2 * which:2 * which + 2, :], writes=[lnbc])

        def layer_norm_rows(xt, s, which_loaded):
            X = xt.t
            for hh in range(2):
                fw.op("vector", lambda h, hh=hh: h.bn_stats(stats.t[:, hh * 6:(hh + 1) * 6], X[:, s, hh * 512:(hh + 1) * 512]),
                      reads=[xt], writes=[stats])
            fw.op("vector", lambda h: h.bn_aggr(small_ln.t[:, c_mean:c_mean + 2], stats.t[:, 0:12]), reads=[stats], writes=[small_ln])
            fw.op("scalar", lambda h: h.activation(scl(c_rstd), scl(c_var), AF.Sqrt, bias=cv_eps(), scale=1.0),
                  reads=[small_ln, epsT], writes=[small_ln])
            fw.op("vector", lambda h: h.reciprocal(scl(c_rstd), scl(c_rstd)), reads=[small_ln], writes=[small_ln])
            fw.op("vector", lambda h: h.tensor_scalar(X[:, s, :], X[:, s, :], scl(c_mean), scl(c_rstd), op0=ALU.subtract, op1=ALU.mult),
                  reads=[xt, small_ln], writes=[xt])
            fw.op("gpsimd", lambda h: h.tensor_tensor(X[:, s, :], X[:, s, :], lnbc.t[:, 0, :], op=ALU.mult), reads=[xt, lnbc], writes=[xt])
            fw.op("gpsimd", lambda h: h.tensor_tensor(X[:, s, :], X[:, s, :], lnbc.t[:, 1, :], op=ALU.add), reads=[xt, lnbc], writes=[xt])

        epsT = fw.sb("epsT", [128, 1], F32)
        fw.op("gpsimd", lambda h: h.memset(epsT.t[:, :], LN_EPS), writes=[epsT])

        def cv_eps():
            return epsT.t[:, 0:1]

        x1T_view = s4a.t[:, :, :].rearrange("p a b -> p (a b)").bitcast(BF16).rearrange("p (k t) -> p k t", t=G)

        def transposes_to(xt, dst_view, dst_tile):
            for s in range(NSUB):
                for b in range(2):
                    p = nextpx()
                    for kk in range(4):
                        kc = 4 * b + kk
                        fw.op("tensor", lambda h, p=p, kk=kk, kc=kc, s=s: h.transpose(
                            p.t[:, kk * 128:(kk + 1) * 128], xt.t[:, s, kc * 128:(kc + 1) * 128], identF.t[:, :]),
                            reads=[xt, identF], writes=[p])
                    fw.op("scalar", lambda h, p=p, b=b, s=s: h.copy(
                        dst_view[:, 4 * b:4 * b + 4, s * 128:(s + 1) * 128],
                        p.t[:, 0:512].rearrange("p (a c) -> p a c", c=128)), reads=[p], writes=[dst_tile])

        def fm_matmul(p, wch, b, ncols=G, rhs_t=None, rhs_tile=None, nk=8, col0=0):
            for kc in range(nk):
                fw.op("tensor", lambda h, kc=kc: h.matmul(
                    p.t[:, col0:col0 + ncols], wch.t[:, (b * nk + kc) * 128:(b * nk + kc + 1) * 128],
                    rhs_t[:, kc, :], start=(kc == 0), stop=(kc == nk - 1)),
                    reads=[wch, rhs_tile], writes=[p])

        def rope_proj(g, c0, dst_t, dst_fn):
            wch = next_chunk(c0)
            for j in range(4):
                p = nextpx()
                fm_matmul(p, wch, j, rhs_t=xT.t, rhs_tile=xT)
                fw.op("vector", lambda h, p=p, j=j: h.tensor_tensor(s4a.t[:, j, :], p.t[:, 0:G], ropeFg.t[:, 0, :], op=ALU.mult),
                      reads=[p, ropeFg], writes=[s4a])
                yield
            wch = next_chunk(c0 + 1)
            for j in range(4):
                p = nextpx()
                fm_matmul(p, wch, j, rhs_t=xT.t, rhs_tile=xT)
                fw.op("vector", lambda h, p=p, j=j: h.tensor_tensor(rtmp.t[:, :], p.t[:, 0:G], ropeFg.t[:, 1, :], op=ALU.mult),
                      reads=[p, ropeFg], writes=[rtmp])
                fw.op("gpsimd", lambda h, j=j: h.tensor_tensor(dst_fn(j), s4a.t[:, j, :], rtmp.t[:, :], op=ALU.add),
                      reads=[s4a, rtmp], writes=[dst_t])
                yield

        def P_a(g):
            xt = xg[g % 2]
            t0 = g * G
            transposes_to(xt, xT.t, xT)
            run(rope_proj(g, 6, qiT, lambda j: qiT.t[:, j, :]))
            wch = next_chunk(9)
            for s in range(NSUB):
                p = nextpx()
                for kc in range(8):
                    fw.op("tensor", lambda h, p=p, kc=kc, s=s, wch=wch: h.matmul(
                        p.t[:, 0:72], xT.t[:, kc, s * 128:(s + 1) * 128], wch.t[:, kc * 72:(kc + 1) * 72],
                        start=(kc == 0), stop=(kc == 7)), reads=[xT, wch], writes=[p])
                fw.op("vector", lambda h, p=p, s=s: h.tensor_scalar(wsc.t[:, s, :], p.t[:, 64:72], float(8.0 ** -0.5 * 0.125), None, op0=ALU.mult),
                      reads=[p], writes=[wsc])
                fw.op("vector", lambda h, p=p: h.bn_stats(stats.t[:, 0:6], p.t[:, 0:64]), reads=[p], writes=[stats])
                fw.op("vector", lambda h: h.bn_aggr(small_ln.t[:, c_mean:c_mean + 2], stats.t[:, 0:6]), reads=[stats], writes=[small_ln])
                fw.op("scalar", lambda h: h.activation(scl(c_rstd), scl(c_var), AF.Sqrt, bias=cv_eps(), scale=1.0),
                      reads=[small_ln, epsT], writes=[small_ln])
                fw.op("vector", lambda h: h.reciprocal(scl(c_rstd), scl(c_rstd)), reads=[small_ln], writes=[small_ln])
                fw.op("vector", lambda h, p=p: h.tensor_scalar(kn.t[:, :], p.t[:, 0:64], scl(c_mean), scl(c_rstd),
                                                               op0=ALU.subtract, op1=ALU.mult), reads=[p, small_ln], writes=[kn])
                fw.op("vector", lambda h: h.tensor_tensor(kn.t[:, :], kn.t[:, :], idxbc.t[:, 0, :], op=ALU.mult), reads=[kn, idxbc], writes=[kn])
                fw.op("vector", lambda h: h.tensor_tensor(kn.t[:, :], kn.t[:, :], idxbc.t[:, 1, :], op=ALU.add), reads=[kn, idxbc], writes=[kn])
                fw.op("vector", lambda h, s=s: h.tensor_tensor(kA.t[:, :], kn.t[:, :], ropeTg.t[:, s, 0:64], op=ALU.mult),
                      reads=[kn, ropeTg], writes=[kA])
                fw.op("vector", lambda h, s=s: h.tensor_tensor(kB.t[:, 0:32], kn.t[:, 32:64], ropeTg.t[:, s, 64:96], op=ALU.mult),
                      reads=[kn, ropeTg], writes=[kB])
                fw.op("vector", lambda h, s=s: h.tensor_tensor(kB.t[:, 32:64], kn.t[:, 0:32], ropeTg.t[:, s, 96:128], op=ALU.mult),
                      reads=[kn, ropeTg], writes=[kB])
                fw.op("vector", lambda h: h.tensor_tensor(kr2.t[:, 0:64], kA.t[:, :], kB.t[:, :], op=ALU.add), reads=[kA, kB], writes=[kr2])
                fw.op("vector", lambda h: h.tensor_tensor(kr2.t[:, 64:128], kA.t[:, :], kB.t[:, :], op=ALU.add), reads=[kA, kB], writes=[kr2])
                p2 = nextpx()
                fw.op("tensor", lambda h, p2=p2: h.transpose(p2.t[:, 0:128], kr2.t[:, :], identF.t[:, :]), reads=[kr2, identF], writes=[p2])
                fw.op("scalar", lambda h, p2=p2, s=s: h.copy(kiT.t[:, t0 + s * 128:t0 + (s + 1) * 128], p2.t[:, 0:128]),
                      reads=[p2], writes=[kiTd[g]])

        def P_b(g):
            xt = xg[g % 2]
            t0 = g * G
            wch = next_chunk(0)
            for j in range(4):
                p = nextpx()
                fm_matmul(p, wch, j, rhs_t=xT.t, rhs_tile=xT)
                fw.op("scalar", lambda h, p=p, j=j: h.activation(s4a.t[:, j, :], p.t[:, 0:G], AF.Sigmoid),
                      reads=[p], writes=[s4a])
                yield
            if g > 0:
                fw.op("gpsimd", lambda h: h.tensor_copy(u.t[:, :, 0:30], u.t[:, :, G:G + 30]), reads=[u], writes=[u])
            wch = next_chunk(1)
            for j in range(4):
                p = nextpx()
                fm_matmul(p, wch, j, rhs_t=xT.t, rhs_tile=xT)
                fw.op("vector", lambda h, p=p, j=j: h.tensor_tensor(u.t[:, j, 30:30 + G], p.t[:, 0:G], s4a.t[:, j, :], op=ALU.mult),
                      reads=[p, s4a], writes=[u])
                yield
            for _ in rope_proj(g, 2, qT, lambda j: qT.t[:, j, :]):
                yield
            for _ in rope_proj(g, 4, KTd[g], lambda j: KT.t[:, j, t0:t0 + G]):
                yield
            wch = next_chunk(8)
            for s in range(NSUB):
                p = nextpx()
                for kc in range(8):
                    fw.op("tensor", lambda h, p=p, kc=kc, s=s, wch=wch: h.matmul(
                        p.t[:, 0:512], xT.t[:, kc, s * 128:(s + 1) * 128], wch.t[:, kc * 512:(kc + 1) * 512],
                        start=(kc == 0), stop=(kc == 7)), reads=[xT, wch], writes=[p])
                kt = g * NSUB + s
                fw.op("scalar", lambda h, p=p, kt=kt: h.copy(
                    Vp.t[:, kt, :, 0:64], p.t[:, 0:512].rearrange("p (a c) -> p a c", c=64)), reads=[p], writes=[Vpd[g]])
                yield
            if g + 1 < NG:
                load_tabs(g + 1)

        def CIA(g):

            def conv_taps():
                for j in range(4):
                    wch = next_chunk(36 + j)
                    pc = nextpx()
                    for i in range(31):
                        fw.op("tensor", lambda h, j=j, i=i, pc=pc, wch=wch: h.matmul(
                            pc.t[:, 0:G], wch.t[:, i * 128:(i + 1) * 128], u.t[:, j, i:i + G], start=(i == 0), stop=(i == 30)),
                            reads=[wch, u], writes=[pc])
                    fw.op("scalar", lambda h, j=j, pc=pc: h.activation(acc.t[:, j, :], pc.t[:, 0:G], AF.Identity, bias=cv(124 + j), scale=1.0),
                          reads=[pc, cvec], writes=[acc])
                    yield

            def conv_ln():
                pst = nextpx()
                for j in range(4):
                    fw.op("scalar", lambda h, j=j: h.activation(s4a.t[:, j, :], acc.t[:, j, :], AF.Square), reads=[acc], writes=[s4a])
                for j in range(4):
                    fw.op("tensor", lambda h, j=j: h.matmul(pst.t[:, 0:G], onesM.t[:, :], acc.t[:, j, :], start=(j == 0), stop=(j == 3)),
                          reads=[onesM, acc], writes=[pst])
                for j in range(4):
                    fw.op("tensor", lambda h, j=j: h.matmul(pst.t[:, G:2 * G], onesM.t[:, :], s4a.t[:, j, :], start=(j == 0), stop=(j == 3)),
                          reads=[onesM, s4a], writes=[pst])
                fw.op("scalar", lambda h: h.copy(cmean.t[:, :], pst.t[:, 0:G]), reads=[pst], writes=[cmean])
                fw.op("vector", lambda h: h.tensor_tensor(crstd.t[:, :], cmean.t[:, :], cmean.t[:, :], op=ALU.mult), reads=[cmean], writes=[crstd])
                fw.op("vector", lambda h: h.tensor_tensor(crstd.t[:, :], pst.t[:, G:2 * G], crstd.t[:, :], op=ALU.subtract),
                      reads=[pst, crstd], writes=[crstd])
                fw.op("scalar", lambda h: h.activation(crstd.t[:, :], crstd.t[:, :], AF.Sqrt, bias=cv_eps(), scale=1.0),
                      reads=[crstd, epsT], writes=[crstd])
                fw.op("vector", lambda h: h.reciprocal(crstd.t[:, :], crstd.t[:, :]), reads=[crstd], writes=[crstd])
                for j in range(4):
                    fw.op("gpsimd", lambda h, j=j: h.tensor_tensor(acc.t[:, j, :], acc.t[:, j, :], cmean.t[:, :], op=ALU.subtract),
                          reads=[acc, cmean], writes=[acc])
                    fw.op("gpsimd", lambda h, j=j: h.tensor_tensor(acc.t[:, j, :], acc.t[:, j, :], crstd.t[:, :], op=ALU.mult),
                          reads=[acc, crstd], writes=[acc])
                    fw.op("scalar", lambda h, j=j: h.activation(convact.t[:, j, :], acc.t[:, j, :], AF.Silu, bias=cv(132 + j), scale=cv(128 + j)),
                          reads=[acc, cvec], writes=[convact])

            def indexer_a(s):
                qt = g * NSUB + s
                n = (qt + 1) * 128
                nchunks = (n + 511) // 512
                it = 0
                for c in range(nchunks):
                    wc = min(512, n - 512 * c)
                    kdeps = [kiTd[gg] for gg in range((512 * c) // G, (512 * c + wc - 1) // G + 1)]
                    for hd in range(8):
                        p = nextpx()
                        rl = relu_ts[it % 2]
                        it += 1
                        r0 = 64 * (hd % 2)
                        fw.op("tensor", lambda h, p=p, hd=hd, r0=r0, c=c, wc=wc: h.matmul(
                            p.t[:, 0:wc], qiT.t[r0:r0 + 64, hd // 2, s * 128:(s + 1) * 128],
                            kiT.t[r0:r0 + 64, 512 * c:512 * c + wc], start=True, stop=True),
                            reads=[qiT] + kdeps, writes=[p])
                        fw.op("scalar", lambda h, p=p, wc=wc, rl=rl: h.activation(rl.t[:, 0:wc], p.t[:, 0:wc], AF.Relu),
                              reads=[p], writes=[rl])
                        if hd == 0:
                            fw.op("vector", lambda h, c=c, wc=wc, rl=rl: h.tensor_scalar(
                                iscore.t[:, 512 * c:512 * c + wc], rl.t[:, 0:wc], wsc.t[:, s, 0:1], None, op0=ALU.mult),
                                reads=[rl, wsc], writes=[iscore])
                        else:
                            fw.op("vector", lambda h, c=c, wc=wc, hd=hd, rl=rl: h.scalar_tensor_tensor(
                                iscore.t[:, 512 * c:512 * c + wc], rl.t[:, 0:wc], wsc.t[:, s, hd:hd + 1],
                                iscore.t[:, 512 * c:512 * c + wc], op0=ALU.mult, op1=ALU.add),
                                reads=[rl, wsc, iscore], writes=[iscore])
                        yield
                fw.op("vector", lambda h: h.memset(iscore.t[0:64, n - 64:n], NEG), writes=[iscore])
                yield

            def bisect(s):
                qt = g * NSUB + s
                n = (qt + 1) * 128
                if n - 64 > topk:
                    fw.op("vector", lambda h: h.tensor_reduce(sc(c_mx), iscore.t[:, 0:n], axis=AX.X, op=ALU.max),
                          reads=[iscore], writes=[small])
                    fw.op("vector", lambda h: h.tensor_reduce(sc(c_lo), iscore.t[:, 0:n - 64], axis=AX.X, op=ALU.min),
                          reads=[iscore], writes=[small])
                    fw.op("vector", lambda h: h.tensor_tensor(sc(c_rng), sc(c_mx), sc(c_lo), op=ALU.subtract), reads=[small], writes=[small])
                    fw.op("vector", lambda h: h.tensor_scalar(steps.t[:, :], cvec.t[:, 152:152 + nb], sc(c_rng), None, op0=ALU.mult),
                          reads=[small, cvec], writes=[steps])
                    nd = max(16, int(round(n * 0.46 / 16.0)) * 16)
                    n_act = n - nd
                    thrc = float(2 * topk - 1 - n_act)
                    ht_ap = half_thr(thrc)
                    nb_t = max(16, nb - int(np.floor(np.log2(4096.0 / n)))) if n < 4096 else nb
                    fw.op("vector", lambda h: h.tensor_tensor(t_mid.t[:, :], sc(c_lo), steps.t[:, 0:1], op=ALU.add),
                          reads=[small, steps], writes=[t_mid])
                    for k in range(nb_t):
                        fw.op("vector", lambda h: h.tensor_scalar(junk.t[:, 0:nd], iscore.t[:, 0:nd], t_mid.t[:, 0:1], None,
                                                                  op0=ALU.is_ge, op1=ALU.add, accum_out=t_cd.t[:, 0:1]),
                              reads=[iscore, t_mid], writes=[junk_lo, t_cd])
                        fw.op("scalar", lambda h: h.activation(junk.t[:, nd:n], iscore.t[:, nd:n], AF.Sign, bias=t_mid.t[:, 0:1], scale=-1.0,
                                                               accum_out=t_sg.t[:, 0:1]),
                              reads=[iscore, t_mid], writes=[junk_hi, t_sg])
                        fw.op("scalar", lambda h: h.activation(t_t.t[:, :], t_sg.t[:, :], AF.Identity, bias=ht_ap, scale=0.5),
                              reads=[t_sg, cvec2], writes=[t_t])
                        fw.op("vector", lambda h, k=k: h.scalar_tensor_tensor(t_d.t[:, :], t_cd.t[:, :], t_t.t[:, 0:1], steps.t[:, k:k + 1],
                                                                               op0=ALU.is_ge, op1=ALU.mult),
                              reads=[t_cd, t_t, steps], writes=[t_d])
                        if k < nb_t - 1:
                            fw.op("vector", lambda h, k=k: h.scalar_tensor_tensor(t_mid.t[:, :], t_d.t[:, :], steps.t[:, k + 1:k + 2], t_mid.t[:, :],
                                                                                   op0=ALU.subtract, op1=ALU.add),
                                  reads=[t_d, steps, t_mid], writes=[t_mid])
                        else:
                            fw.op("vector", lambda h, k=k: h.scalar_tensor_tensor(sc(c_lo), t_d.t[:, :], steps.t[:, k:k + 1], t_mid.t[:, :],
                                                                                   op0=ALU.subtract, op1=ALU.add),
                                  reads=[t_d, steps, t_mid], writes=[small])
                        yield
                else:
                    fw.op("vector", lambda h: h.memset(sc(c_lo), -1.0e29), writes=[small])
                    yield

            def indexer_b(s):
                qt = g * NSUB + s
                mT = maskTd[qt % 2]
                mTbase = (qt % 2) * 4096
                nkt = qt + 1
                for kb in range((nkt + 7) // 8):
                    k0 = kb * 8
                    k1 = min(nkt, k0 + 8)
                    w8 = (k1 - k0) * 128
                    fw.op("vector", lambda h, k0=k0, w8=w8: h.tensor_scalar(mask8.t[:, 0:w8], iscore.t[:, k0 * 128:k0 * 128 + w8],
                                                                            sc(c_lo), -30000.0, op0=ALU.is_lt, op1=ALU.mult),
                          reads=[iscore, small], writes=[mask8])
                    for kk in range(k1 - k0):
                        fw.op("tensor", lambda h, kk=kk: h.transpose(pTb.t[:, kk * 128:(kk + 1) * 128], mask8.t[:, kk * 128:(kk + 1) * 128],
                                                                     identB.t[:, :]), reads=[mask8, identB], writes=[pTb])
                    fw.op("scalar", lambda h, k0=k0, w8=w8: h.copy(big16.t[:, mTbase + k0 * 128:mTbase + k0 * 128 + w8], pTb.t[:, 0:w8]),
                          reads=[pTb], writes=[mT])

            def attention(s):
                qt = g * NSUB + s
                mT = maskTd[qt % 2]
                mTbase = (qt % 2) * 4096
                nkt = qt + 1

                def emit_S(kt):
                    gk = kt // NSUB
                    banks = [pX[2 * (kt % 2)], pX[2 * (kt % 2) + 1]]
                    for hd in range(8):
                        r0 = 64 * (hd % 2)
                        bk = banks[hd % 2]
                        fw.op("tensor", lambda h, hd=hd, r0=r0, bk=bk: h.matmul(
                            bk.t[:, (hd // 2) * 128:(hd // 2 + 1) * 128],
                            KT.t[r0:r0 + 64, hd // 2, kt * 128:(kt + 1) * 128],
                            qT.t[r0:r0 + 64, hd // 2, s * 128:(s + 1) * 128], start=(hd // 2 == 0), stop=False,
                            skip_group_check=True), reads=[KTd[gk], qT], writes=[bk])
                    for hb in range(2):
                        bk = banks[hb]
                        fw.op("tensor", lambda h, bk=bk: h.matmul(
                            bk.t[:, 0:512].rearrange("p (a c) -> p a c", c=128), identB.t[:, :],
                            big16.t[:, mTbase + kt * 128:mTbase + (kt + 1) * 128].unsqueeze(1).broadcast_to([128, 4, 128]),
                            start=False, stop=True, skip_group_check=True), reads=[identB, mT], writes=[bk])

                def emit_exp(kt):
                    banks = [pX[2 * (kt % 2)], pX[2 * (kt % 2) + 1]]
                    pts = [PT[(2 * kt) % 4], PT[(2 * kt + 1) % 4]]
                    for hb in range(2):
                        fw.op("scalar", lambda h, hb=hb: h.activation(pts[hb].t[:, :], banks[hb].t[:, :], AF.Exp, scale=0.125),
                              reads=[banks[hb]], writes=[pts[hb]])

                def emit_pv(kt):
                    gk = kt // NSUB
                    pts = [PT[(2 * kt) % 4], PT[(2 * kt + 1) % 4]]
                    for hd in range(8):
                        ob = pO[hd // 4]
                        fw.op("tensor", lambda h, hd=hd, ob=ob: h.matmul(
                            ob.t[:, (hd % 4) * 65:(hd % 4) * 65 + 65],
                            pts[hd % 2].t[:, (hd // 2) * 128:(hd // 2 + 1) * 128],
                            Vp.t[:, kt, hd, :], start=(kt == 0 and hd % 4 == 0), stop=(kt == nkt - 1),
                            skip_group_check=True), reads=[pts[hd % 2], Vpd[gk]], writes=[ob])

                emit_S(0)
                emit_exp(0)
                for kt in range(nkt):
                    if kt + 1 < nkt:
                        emit_S(kt + 1)
                        emit_exp(kt + 1)
                    emit_pv(kt)
                    yield

            def attention_tail(s):
                for b in range(2):
                    fw.op("vector", lambda h, b=b: h.reciprocal(rden.t[:, 4 * b:4 * b + 4], pO[b].t[:, 64:260:65]), reads=[pO[b]], writes=[rden])
                    fw.op("vector", lambda h, b=b: h.tensor_tensor(
                        attn.t[:, 256 * b:256 * (b + 1)].rearrange("p (a c) -> p a c", c=64),
                        pO[b].t[:, 0:260].rearrange("p (a c) -> p a c", c=65)[:, :, 0:64],
                        rden.t[:, 4 * b:4 * b + 4].unsqueeze(2).broadcast_to([128, 4, 64]), op=ALU.mult),
                        reads=[pO[b], rden], writes=[attn])
                for j in range(4):
                    fw.op("tensor", lambda h, j=j: h.transpose(pTb.t[:, j * 128:(j + 1) * 128], attn.t[:, j * 128:(j + 1) * 128], identB.t[:, :]),
                          reads=[attn, identB], writes=[pTb])
                fw.op("scalar", lambda h: h.copy(attnT.t[:, :, s * 128:(s + 1) * 128],
                                                 pTb.t[:, 0:512].rearrange("p (a c) -> p a c", c=128)), reads=[pTb], writes=[attnT])

            return dict(conv_taps=conv_taps, conv_ln=conv_ln, indexer_a=indexer_a, bisect=bisect, indexer_b=indexer_b, attention=attention,
                        attention_tail=attention_tail)

        def M_step(g):
            xt = xg[g % 2]
            for j in range(8):
                wM = next_chunk(10 + j)
                pg = nextpx()
                fm_matmul(pg, wM, 0, rhs_t=xT.t, rhs_tile=xT, col0=0)
                fm_matmul(pg, wM, 1, rhs_t=xT.t, rhs_tile=xT, col0=G)
                po = nextpx()
                fm_matmul(po, wM, 4, rhs_t=convact.t, rhs_tile=convact, nk=4, col0=0)
                fm_matmul(po, wM, 5, rhs_t=attnT.t, rhs_tile=attnT, nk=4, col0=G)
                fw.op("scalar", lambda h, pg=pg, j=j: h.activation(s4a.t[:, 0, :], pg.t[:, 0:G], AF.Sigmoid, bias=cv(136 + j), scale=1.0),
                      reads=[pg, cvec], writes=[s4a])
                fw.op("scalar", lambda h, pg=pg, j=j: h.activation(s4a.t[:, 1, :], pg.t[:, G:2 * G], AF.Sigmoid, bias=cv(144 + j), scale=1.0),
                      reads=[pg, cvec], writes=[s4a])
                fw.op("vector", lambda h, po=po: h.tensor_tensor(s4a.t[:, 2, :], po.t[:, 0:G], s4a.t[:, 0, :], op=ALU.mult),
                      reads=[po, s4a], writes=[s4a])
                fw.op("vector", lambda h, po=po: h.tensor_tensor(s4a.t[:, 3, :], po.t[:, G:2 * G], s4a.t[:, 1, :], op=ALU.mult),
                      reads=[po, s4a], writes=[s4a])
                fw.op("gpsimd", lambda h, j=j: h.tensor_tensor(mergedT.t[:, j, :], s4a.t[:, 2, :], s4a.t[:, 3, :], op=ALU.add),
                      reads=[s4a], writes=[mergedT])
            for hh in range(2):
                wch = next_chunk(18 + hh)
                for s in range(NSUB):
                    p = nextpx()
                    for kc in range(8):
                        fw.op("tensor", lambda h, p=p, kc=kc, s=s, wch=wch: h.matmul(
                            p.t[:, 0:512], mergedT.t[:, kc, s * 128:(s + 1) * 128], wch.t[:, kc * 512:(kc + 1) * 512],
                            start=(kc == 0), stop=(kc == 7)), reads=[mergedT, wch], writes=[p])
                    fw.op("vector", lambda h, p=p, s=s, hh=hh: h.scalar_tensor_tensor(
                        xt.t[:, s, hh * 512:(hh + 1) * 512], xt.t[:, s, hh * 512:(hh + 1) * 512], float(ALPHA), p.t[:, 0:512],
                        op0=ALU.mult, op1=ALU.add), reads=[xt, p], writes=[xt])
            for s in range(NSUB):
                layer_norm_rows(xt, s, 0)
            load_ln(1)

        def F_step(g):
            xt = xg[g % 2]
            t0 = g * G
            transposes_to(xt, x1T_view, s4a)
            yield
            hdeps = maskTd
            for c in range(8):
                wch = next_chunk(20 + c)
                for b in range(4):
                    blk = 4 * c + b
                    p = nextpx()
                    fm_matmul(p, wch, b, rhs_t=x1T_view, rhs_tile=s4a)
                    fw.op("scalar", lambda h, p=p: h.activation(rtmp.t[:, 0:G], p.t[:, 0:G], AF.Relu), reads=[p], writes=[rtmp])
                    fw.op("gpsimd", lambda h, blk=blk: h.tensor_tensor(big16.t[:, blk * G:(blk + 1) * G], rtmp.t[:, 0:G], rtmp.t[:, 0:G], op=ALU.mult),
                          reads=[rtmp], writes=hdeps)
                    yield
            for hh in range(2):
                for c in range(4):
                    wch = next_chunk(28 + 4 * hh + c)
                    for s in range(NSUB):
                        for jj in range(8):
                            j = 8 * c + jj
                            fw.op("tensor", lambda h, s=s, j=j, jj=jj, wch=wch: h.matmul(
                                pO[s].t[:, 0:512], big16.t[:, j * G + s * 128:j * G + (s + 1) * 128], wch.t[:, jj * 512:(jj + 1) * 512],
                                start=(j == 0), stop=(j == 31)), reads=hdeps + [wch], writes=[pO[s]])
                        yield
                for s in range(NSUB):
                    fw.op("vector", lambda h, s=s, hh=hh: h.scalar_tensor_tensor(
                        xt.t[:, s, hh * 512:(hh + 1) * 512], xt.t[:, s, hh * 512:(hh + 1) * 512], float(ALPHA), pO[s].t[:, 0:512],
                        op0=ALU.mult, op1=ALU.add), reads=[xt, pO[s]], writes=[xt])
                yield
            for s in range(NSUB):
                layer_norm_rows(xt, s, 1)
                yield
            if g + 1 < NG:
                load_ln(0)
            out_waits.append(fw.dma("gpsimd", lambda: out_d[t0:t0 + G, :].rearrange("(s p) d -> p s d", p=128),
                                    lambda: xt.t[:, :, :], reads=[xt], pe=out_pes[g % 2]))

        def run(gen):
            for _ in gen:
                pass

        def interleave(ga, gb):
            da = db = False
            while not (da and db):
                if not da:
                    try:
                        next(ga)
                    except StopIteration:
                        da = True
                if not db:
                    try:
                        next(gb)
                    except StopIteration:
                        db = True

        def chain(*gens):
            for gg_ in gens:
                for _ in gg_:
                    yield

        def interleave_until(ga, gb, na=1, nb=1):
            while True:
                for _ in range(na):
                    try:
                        next(ga)
                    except StopIteration:
                        return
                for _ in range(nb):
                    try:
                        next(gb)
                    except StopIteration:
                        break

        def CIA_first(g, fns, filler):
            interleave_until(fns["indexer_a"](0), filler, na=3, nb=1)
            interleave_until(fns["bisect"](0), filler, na=1, nb=2)
            run(filler)

        def CIA_rest(g, fns):
            fns["indexer_b"](0)
            filler = chain(fns["attention"](0), fns["conv_taps"]())
            interleave_until(fns["indexer_a"](1), filler, na=4, nb=1)
            interleave_until(fns["bisect"](1), filler, na=1, nb=2)
            run(filler)
            fns["attention_tail"](0)
            fns["conv_ln"]()
            fns["indexer_b"](1)
            run(fns["attention"](1))
            fns["attention_tail"](1)

        assert NSUB == 2
        load_x(0)
        load_tabs(0)
        load_ln(0)
        P_a(0)
        fns = CIA(0)
        CIA_first(0, fns, P_b(0))
        CIA_rest(0, fns)
        for g in range(NG):
            if g + 1 < NG:
                load_x(g + 1)
            M_step(g)
            if g + 1 < NG:
                P_a(g + 1)
                fns = CIA(g + 1)
                CIA_first(g + 1, fns, chain(F_step(g), P_b(g + 1)))
                CIA_rest(g + 1, fns)
            else:
                run(F_step(g))
        fw.finish(out_waits)
    return nc


_PROG_CACHE = {}


def run_cores(xs, consts, wall, L, topk):
    key = (L, topk)
    if key not in _PROG_CACHE:
        _PROG_CACHE[key] = build_program(L, topk)
    nc = _PROG_CACHE[key]
    in_maps = []
    for xb in xs:
        m = {"x": np.ascontiguousarray(xb, dtype=np.float32), "wall": wall}
        m.update(consts)
        in_maps.append(m)
    res = run_bass_kernel_spmd(nc, in_maps, core_ids=list(range(len(xs))))
    return [r["out"] for r in res.results]


def kernel(**inputs):
    x = np.asarray(inputs["x"], np.float32)
    B, L, _ = x.shape
    topk = min(256, L // 4)
    wall = host_weights(inputs)
    consts = host_consts(inputs, L)
    outs = run_cores([x[b] for b in range(B)], consts, wall, L, topk)
    return np.stack(outs, axis=0).astype(np.float32)
```

```python
import numpy as np
from contextlib import ExitStack
import concourse.bass as bass
import concourse.mybir as mybir
from concourse.bass_utils import run_bass_kernel_spmd

F32 = mybir.dt.float32
BF16 = mybir.dt.bfloat16
U8 = mybir.dt.uint8
AF = mybir.ActivationFunctionType
ALU = mybir.AluOpType
AX = mybir.AxisListType

D = 1024
NB_BISECT = 22
G = 256
NSUB = G // 128
CHW = 4096
NSLOT = 3
ALPHA = 2.0 ** 0.25
LN_EPS = 1e-5
NEG = -1.0e30
N_CHUNK = 40


class Eng:
    def __init__(self, name, sem, same_sync):
        self.name = name
        self.sem = sem
        self.count = 0
        self.waited = {}
        self.ops = []
        self.same_sync = same_sync


class Rec:
    def __init__(self):
        self.call = None

    def __getattr__(self, name):
        def f(*a, **k):
            self.call = (name, a, k)
            return self
        return f


class Dep:
    def __init__(self, name="dep"):
        self.name = name
        self.last_write = None
        self.reads = []
        self.dma_eng = None


class Tile(Dep):
    def __init__(self, name, t, excl=False):
        Dep.__init__(self, name)
        self.t = t
        self.excl = excl

    def __getitem__(self, idx):
        return self.t[idx]


class FW:
    def __init__(self, nc, stack):
        self.nc = nc
        self.stack = stack
        self.engs = {}
        self.muted = False
        for n in ["tensor", "vector", "scalar", "gpsimd", "sync"]:
            sem = stack.enter_context(nc.semaphore("sem_" + n))
            self.engs[n] = Eng(n, sem, same_sync=(n in ("vector", "scalar", "gpsimd")))

    def sb(self, name, shape, dt):
        return Tile(name, self.stack.enter_context(self.nc.sbuf_tensor("sb_" + name, shape, dt)))

    def ps(self, name, shape, dt):
        return Tile(name, self.stack.enter_context(self.nc.psum_tensor("ps_" + name, shape, dt)), excl=True)

    def dma_eng(self, name):
        sem = self.stack.enter_context(self.nc.semaphore("dsem_" + name))
        return Eng("dma_" + name, sem, False)

    def _deps(self, eng, reads, writes):
        deps = []
        for t in reads:
            if t.last_write is not None:
                deps.append(t.last_write)
        for t in writes:
            if t.last_write is not None:
                deps.append(t.last_write)
            deps.extend(t.reads)
        waits = {}
        for (e2, seq) in deps:
            if e2 is eng and not eng.same_sync:
                continue
            if eng.waited.get(e2, 0) >= seq:
                continue
            if waits.get(e2, 0) < seq:
                waits[e2] = seq
        for e2, seq in waits.items():
            eng.waited[e2] = seq
        return list(waits.items())

    def _mark(self, who, seq, reads, writes):
        for t in reads:
            t.reads.append((who, seq))
        for t in writes:
            t.last_write = (who, seq)
            t.reads = []

    def op(self, engname, fn, reads=(), writes=()):
        if self.muted:
            return
        eng = self.engs[engname]
        ex = [t for t in reads if getattr(t, "excl", False)]
        if ex:
            writes = list(writes) + [t for t in ex if t not in writes]
        waits = self._deps(eng, reads, writes)
        eng.count += 1
        seq = eng.count
        rec = Rec()
        fn(rec)
        mname, margs, mkw = rec.call

        def thunk(h, waits=waits, eng=eng, mname=mname, margs=margs, mkw=mkw):
            for e2, s in waits:
                h.wait_ge(e2.sem, s)
            getattr(h, mname)(*margs, **mkw).then_inc(eng.sem, 1)

        eng.ops.append(thunk)
        self._mark(eng, seq, reads, writes)

    def dma(self, qname, out_fn, in_fn, reads=(), writes=(), pe=None):
        if self.muted:
            return None
        q = self.engs[qname]
        waits = self._deps(q, reads, writes)
        if pe is None:
            key = writes[0] if writes else reads[0]
            if key.dma_eng is None:
                key.dma_eng = self.dma_eng(key.name)
            pe = key.dma_eng
        pe.count += 16
        seq = pe.count
        o_ap = out_fn()
        i_ap = in_fn()

        def thunk(h, waits=waits, pe=pe, o_ap=o_ap, i_ap=i_ap):
            for e2, s in waits:
                h.wait_ge(e2.sem, s)
            h.dma_start(out=o_ap, in_=i_ap).then_inc(pe.sem, 16)

        q.ops.append(thunk)
        self._mark(pe, seq, reads, writes)
        return pe, seq

    def finish(self, final_waits):
        nc = self.nc
        engs = self.engs
        with nc.Block() as block:
            def mk(name):
                eng = engs[name]

                def body(h):
                    for th in eng.ops:
                        th(h)
                    if name == "sync":
                        for pe, s in final_waits:
                            h.wait_ge(pe.sem, s)
                return body
            block.tensor(mk("tensor"))
            block.vector(mk("vector"))
            block.scalar(mk("scalar"))
            block.gpsimd(mk("gpsimd"))
            block.sync(mk("sync"))


def _fm_block(w, cols):
    K = w.shape[0]
    blk = w[:, cols]
    return blk.reshape(K // 128, 128, 128).transpose(1, 0, 2).reshape(128, -1)


def _kc_major(w):
    K, N = w.shape
    return w.reshape(K // 128, 128, N).transpose(1, 0, 2).reshape(128, -1)


def host_weights(inp):
    w_in = np.asarray(inp["w_in"][0], np.float32)
    oa, ob, oq, ok, ov, oqi, oki, owi, ogc, oga = 0, 512, 1024, 1536, 2048, 2560, 3072, 3136, 3144, 4168
    sw = np.arange(512).reshape(8, 64)
    sw = np.concatenate([sw[:, 32:], sw[:, :32]], axis=1).ravel()
    chunks = []

    def pad(a):
        out = np.zeros((128, CHW), np.float32)
        out[:, :a.shape[1]] = a
        return out

    def fm4(cols512):
        return np.concatenate([_fm_block(w_in, cols512[b * 128:(b + 1) * 128]) for b in range(4)], axis=1)

    r = np.arange(512)
    chunks.append(fm4(ob + r))
    chunks.append(fm4(oa + r))
    chunks.append(fm4(oq + r))
    chunks.append(fm4(oq + sw))
    chunks.append(fm4(ok + r))
    chunks.append(fm4(ok + sw))
    chunks.append(fm4(oqi + r))
    chunks.append(fm4(oqi + sw))
    chunks.append(_kc_major(w_in[:, ov:ov + 512]))
    chunks.append(pad(_kc_major(w_in[:, oki:oki + 72])))
    w_co = np.asarray(inp["w_conv_out"][0], np.float32)
    w_ao = np.asarray(inp["w_attn_out"][0], np.float32)
    r128 = np.arange(128)
    for j in range(8):
        blks = [_fm_block(w_in, ogc + j * 128 + r128), _fm_block(w_in, oga + j * 128 + r128),
                _fm_block(w_co, j * 128 + r128), _fm_block(w_ao, j * 128 + r128)]
        chunks.append(pad(np.concatenate(blks, axis=1)))
    w_out = np.asarray(inp["w_out"][0], np.float32)
    for h in range(2):
        chunks.append(_kc_major(w_out[:, h * 512:(h + 1) * 512]))
    w_fi = np.asarray(inp["w_ff_in"][0], np.float32)
    for c in range(8):
        chunks.append(np.concatenate([_fm_block(w_fi, (4 * c + b) * 128 + r128) for b in range(4)], axis=1))
    w_fo = np.asarray(inp["w_ff_out"][0], np.float32)
    for h in range(2):
        for c in range(4):
            chunks.append(_kc_major(w_fo[c * 1024:(c + 1) * 1024, h * 512:(h + 1) * 512]))
    dw_w = np.asarray(inp["dw_w"][0], np.float32)
    for j in range(4):
        dg = np.zeros((128, 31, 128), np.float32)
        idx = np.arange(128)
        dg[idx, :, idx] = dw_w[:, j * 128:(j + 1) * 128].T
        chunks.append(pad(dg.reshape(128, 31 * 128)))
    wall = np.stack(chunks, axis=0)
    assert wall.shape == (N_CHUNK, 128, CHW), wall.shape
    return np.ascontiguousarray(wall)


def host_consts(inp, L):
    nb = NB_BISECT
    cvec = np.zeros((128, 152 + nb), np.float32)
    dw_w = np.asarray(inp["dw_w"][0], np.float32)
    for j in range(4):
        cvec[:, j * 31:(j + 1) * 31] = dw_w[:, j * 128:(j + 1) * 128].T
    cvec[:, 124:128] = np.asarray(inp["dw_b"][0]).reshape(4, 128).T
    cvec[:, 128:132] = np.asarray(inp["conv_ln_g"][0]).reshape(4, 128).T
    cvec[:, 132:136] = np.asarray(inp["conv_ln_b"][0]).reshape(4, 128).T
    gb = np.asarray(inp["gate_b"][0], np.float32)
    cvec[:, 136:144] = gb[0].reshape(8, 128).T
    cvec[:, 144:152] = gb[1].reshape(8, 128).T
    cvec[:, 152:152 + nb] = (0.5 ** np.arange(1, nb + 1))[None, :]
    lnbc = np.stack([np.broadcast_to(np.asarray(inp[k][0], np.float32)[None, :], (128, D))
                     for k in ("ln1_g", "ln1_b", "ln2_g", "ln2_b")], axis=1)
    idxbc = np.stack([np.broadcast_to(np.asarray(inp[k][0], np.float32)[None, :], (128, 64))
                      for k in ("idx_k_ln_g", "idx_k_ln_b")], axis=1)
    inv_freq = (10000.0 ** (-np.arange(0, 64, 2, dtype=np.float32) / np.float32(64))).astype(np.float32)
    ang = (np.arange(L, dtype=np.float32)[:, None] * inv_freq[None, :]).astype(np.float32)
    cos = np.cos(ang.astype(np.float64)).astype(np.float32)
    sin = np.sin(ang.astype(np.float64)).astype(np.float32)
    cos2 = np.concatenate([cos, cos], axis=1)
    sinS = np.concatenate([-sin, sin], axis=1)
    ropeT = np.concatenate([cos2, sinS], axis=1)
    ropeF = np.stack([np.concatenate([cos2.T, cos2.T], axis=0),
                      np.concatenate([sinS.T, sinS.T], axis=0)], axis=1)
    identF = np.eye(128, dtype=np.float32)
    return dict(cvec=cvec, lnbc=np.ascontiguousarray(lnbc), idxbc=np.ascontiguousarray(idxbc),
                ropeT=np.ascontiguousarray(ropeT), ropeF=np.ascontiguousarray(ropeF), identF=identF)


class _Stop(Exception):
    pass


def build_program(L, topk):
    import os
    STOP = os.environ.get('K_STOP', '')
    NGRUN = int(os.environ.get('K_NG', '0'))
    STOPS = int(os.environ.get('K_STOPS', '0'))
    NG = L // G
    NT = L // 128
    nb = NB_BISECT
    NV = 152 + nb
    nc = bass.Bass("TRN2", target_bir_lowering=False)
    x_d = nc.dram_tensor("x", [L, D], F32, kind="ExternalInput").ap()
    wall_d = nc.dram_tensor("wall", [N_CHUNK, 128, CHW], F32, kind="ExternalInput").ap()
    cvec_d = nc.dram_tensor("cvec", [128, NV], F32, kind="ExternalInput").ap()
    lnbc_d = nc.dram_tensor("lnbc", [128, 4, D], F32, kind="ExternalInput").ap()
    idxbc_d = nc.dram_tensor("idxbc", [128, 2, 64], F32, kind="ExternalInput").ap()
    ropeT_d = nc.dram_tensor("ropeT", [L, 128], F32, kind="ExternalInput").ap()
    ropeF_d = nc.dram_tensor("ropeF", [128, 2, L], F32, kind="ExternalInput").ap()
    identF_d = nc.dram_tensor("identF", [128, 128], F32, kind="ExternalInput").ap()
    out_d = nc.dram_tensor("out", [L, D], F32, kind="ExternalOutput").ap()
    wscr_d = nc.dram_tensor("wscr", [N_CHUNK, 128, CHW], BF16).ap()

    with ExitStack() as st:
        fw = FW(nc, st)
        KT = fw.sb("KT", [128, 4, L], BF16)
        Vp = fw.sb("Vp", [128, NT, 8, 65], BF16)
        kiT = fw.sb("kiT", [128, L], BF16)
        KTd = [Dep("KT%d" % g) for g in range(NG)]
        Vpd = [Dep("Vp%d" % g) for g in range(NG)]
        kiTd = [Dep("kiT%d" % g) for g in range(NG)]
        xg = [fw.sb("xg%d" % i, [128, NSUB, D], F32) for i in range(2)]
        xT = fw.sb("xT", [128, 8, G], BF16)
        s4a = fw.sb("s4a", [128, 4, G], F32)
        u = fw.sb("u", [128, 4, 30 + G], BF16)
        acc = fw.sb("acc", [128, 4, G], F32)
        convact = fw.sb("convact", [128, 4, G], BF16)
        rtmp = fw.sb("rtmp", [128, G], F32)
        qT = fw.sb("qT", [128, 4, G], BF16)
        qiT = fw.sb("qiT", [128, 4, G], BF16)
        ropeFg = fw.sb("ropeFg", [128, 2, G], F32)
        ropeTg = fw.sb("ropeTg", [128, NSUB, 128], F32)
        iscore = fw.sb("iscore", [128, max(L, CHW)], F32)
        junk = fw.sb("junk", [128, L], U8)
        mask8 = fw.sb("mask8", [128, 1024], BF16)
        big16 = fw.sb("big16", [128, 8192], BF16)
        maskTd = [Dep("maskT0"), Dep("maskT1")]
        relu_ts = [fw.sb("relu_t%d" % i, [128, 512], F32) for i in range(2)]
        relu_t = relu_ts[0]
        PT = [fw.sb("PT%d" % i, [128, 512], BF16) for i in range(4)]
        attn = fw.sb("attn", [128, 512], BF16)
        attnT = fw.sb("attnT", [128, 4, G], BF16)
        mergedT = fw.sb("mergedT", [128, 8, G], BF16)
        lnbc = fw.sb("lnbc", [128, 2, D], F32)
        idxbc = fw.sb("idxbc", [128, 2, 64], F32)
        cvec = fw.sb("cvec", [128, NV], F32)
        identF = fw.sb("identF", [128, 128], F32)
        identB = fw.sb("identB", [128, 128], BF16)
        onesM = fw.sb("onesM", [128, 128], F32)
        kn = fw.sb("kn", [128, 64], F32)
        kA = fw.sb("kA", [128, 64], F32)
        kB = fw.sb("kB", [128, 64], F32)
        kr2 = fw.sb("kr2", [128, 128], F32)
        wsc = fw.sb("wsc", [128, NSUB, 8], F32)
        small = fw.sb("small", [128, 64], F32)
        steps = fw.sb("steps", [128, nb], F32)
        small_ln = fw.sb("small_ln", [128, 8], F32)
        t_mid = fw.sb("t_mid", [128, 1], F32)
        t_cd = fw.sb("t_cd", [128, 1], F32)
        t_sg = fw.sb("t_sg", [128, 1], F32)
        t_t = fw.sb("t_t", [128, 1], F32)
        t_d = fw.sb("t_d", [128, 1], F32)
        cvec2 = fw.sb("cvec2", [128, 2 * (L // 128) + 4], F32)
        _ht = {}

        def half_thr(v):
            if v not in _ht:
                col = len(_ht)
                fw.op("gpsimd", lambda h: h.memset(cvec2.t[:, col:col + 1], v * 0.5), writes=[cvec2])
                _ht[v] = col
            c_ = _ht[v]
            return cvec2.t[:, c_:c_ + 1]

        junk_lo = Dep("junk_lo")
        junk_hi = Dep("junk_hi")
        stats = fw.sb("stats", [128, 12], F32)
        rden = fw.sb("rden", [128, 8], F32)
        cmean = fw.sb("cmean", [128, G], F32)
        crstd = fw.sb("crstd", [128, G], F32)
        slots = [fw.sb("wslot%d" % i, [128, CHW], BF16) for i in range(NSLOT)]
        NPX = 5
        pX = [fw.ps("pX%d" % i, [128, 512], F32) for i in range(NPX)]
        pO = [fw.ps("pO%d" % i, [128, 512], F32) for i in range(2)]
        pTb = fw.ps("pTb", [128, 1024], BF16)
        pxi = [0]

        def nextpx():
            p = pX[pxi[0] % NPX]
            pxi[0] += 1
            return p

        SM = small.t
        c_mean, c_var, c_rstd, c_lo, c_mid, c_cnt, c_d, c_mx, c_mn, c_rng, c_thr = range(11)

        def sc(i):
            return SM[:, i:i + 1]

        def scl(i):
            return small_ln.t[:, i:i + 1]

        def cv(i):
            return cvec.t[:, i:i + 1]

        fw.dma("sync", lambda: cvec.t[:, :], lambda: cvec_d[:, :], writes=[cvec])
        fw.dma("sync", lambda: identF.t[:, :], lambda: identF_d[:, :], writes=[identF])
        fw.dma("sync", lambda: idxbc.t[:, :, :], lambda: idxbc_d[:, :, :], writes=[idxbc])
        fw.op("vector", lambda h: h.tensor_copy(identB.t[:, :], identF.t[:, :]), reads=[identF], writes=[identB])
        fw.op("gpsimd", lambda h: h.memset(onesM.t[:, :], 1.0 / 512.0), writes=[onesM])
        fw.op("gpsimd", lambda h: h.memset(u.t[:, :, :], 0.0), writes=[u])
        fw.op("gpsimd", lambda h: h.memset(Vp.t[:, :, :, :].rearrange("p a b c -> p (a b) c")[:, :, 64:65], 1.0), writes=Vpd)

        scr_chunk = [Dep("wscr%d" % i) for i in range(N_CHUNK)]
        conv_order = [6, 7, 9, 0, 1, 2, 3, 4, 5, 8] + list(range(36, 40)) + list(range(10, 36))
        assert sorted(conv_order) == list(range(N_CHUNK))
        for i in conv_order:
            fw.dma("gpsimd", lambda i=i: wscr_d[i, :, :], lambda i=i: wall_d[i, :, :], writes=[scr_chunk[i]],
                   pe=fw.dma_eng("wconv%d" % i))

        PA_ORDER = [6, 7, 9]
        PB_ORDER = [0, 1, 2, 3, 4, 5, 8]
        order = PA_ORDER + PB_ORDER + list(range(36, 40))
        for gg in range(NG):
            order += list(range(10, 20))
            if gg + 1 < NG:
                order += PA_ORDER
            order += list(range(20, 36))
            if gg + 1 < NG:
                order += PB_ORDER + list(range(36, 40))
        stream = {"issued": 0, "next": 0}
        total_chunks = len(order)

        def issue_upto(n):
            while stream["issued"] < min(n, total_chunks):
                k = stream["issued"]
                sl = slots[k % NSLOT]
                fw.dma("sync", lambda sl=sl: sl.t[:, :], lambda k=k: wscr_d[order[k], :, :], reads=[scr_chunk[order[k]]], writes=[sl])
                stream["issued"] += 1

        def next_chunk(expect):
            k = stream["next"]
            assert order[k] == expect, (k, order[k], expect)
            issue_upto(k + NSLOT)
            stream["next"] += 1
            return slots[k % NSLOT]

        out_waits = []
        out_pes = [fw.dma_eng("outst%d" % i) for i in range(2)]

        def load_x(g):
            xt = xg[g % 2]
            fw.dma("sync", lambda: xt.t[:, :, :],
                   lambda: x_d[g * G:(g + 1) * G, :].rearrange("(s p) d -> p s d", p=128), writes=[xt])

        def load_tabs(g):
            fw.dma("sync", lambda: ropeFg.t[:, :, :], lambda: ropeF_d[:, :, g * G:(g + 1) * G], writes=[ropeFg])
            fw.dma("sync", lambda: ropeTg.t[:, :, :],
                   lambda: ropeT_d[g * G:(g + 1) * G, :].rearrange("(s p) d -> p s d", p=128), writes=[ropeTg])

        def load_ln(which):
            fw.dma("sync", lambda: lnbc.t[:, :, :], lambda: lnbc_d[:, 2 * which:2 * which + 2, :], writes=[lnbc])

        def layer_norm_rows(xt, s, which_loaded):
            X = xt.t
            for hh in range(2):
                fw.op("vector", lambda h, hh=hh: h.bn_stats(stats.t[:, hh * 6:(hh + 1) * 6], X[:, s, hh * 512:(hh + 1) * 512]),
                      reads=[xt], writes=[stats])
            fw.op("vector", lambda h: h.bn_aggr(small_ln.t[:, c_mean:c_mean + 2], stats.t[:, 0:12]), reads=[stats], writes=[small_ln])
            fw.op("scalar", lambda h: h.activation(scl(c_rstd), scl(c_var), AF.Sqrt, bias=cv_eps(), scale=1.0),
                  reads=[small_ln, epsT], writes=[small_ln])
            fw.op("vector", lambda h: h.reciprocal(scl(c_rstd), scl(c_rstd)), reads=[small_ln], writes=[small_ln])
            fw.op("vector", lambda h: h.tensor_scalar(X[:, s, :], X[:, s, :], scl(c_mean), scl(c_rstd), op0=ALU.subtract, op1=ALU.mult),
                  reads=[xt, small_ln], writes=[xt])
            fw.op("gpsimd", lambda h: h.tensor_tensor(X[:, s, :], X[:, s, :], lnbc.t[:, 0, :], op=ALU.mult), reads=[xt, lnbc], writes=[xt])
            fw.op("gpsimd", lambda h: h.tensor_tensor(X[:, s, :], X[:, s, :], lnbc.t[:, 1, :], op=ALU.add), reads=[xt, lnbc], writes=[xt])

        epsT = fw.sb("epsT", [128, 1], F32)
        fw.op("gpsimd", lambda h: h.memset(epsT.t[:, :], LN_EPS), writes=[epsT])

        def cv_eps():
            return epsT.t[:, 0:1]

        x1T_view = s4a.t[:, :, :].rearrange("p a b -> p (a b)").bitcast(BF16).rearrange("p (k t) -> p k t", t=G)

        def transposes_to(xt, dst_view, dst_tile):
            for s in range(NSUB):
                for b in range(2):
                    p = nextpx()
                    for kk in range(4):
                        kc = 4 * b + kk
                        fw.op("tensor", lambda h, p=p, kk=kk, kc=kc, s=s: h.transpose(
                            p.t[:, kk * 128:(kk + 1) * 128], xt.t[:, s, kc * 128:(kc + 1) * 128], identF.t[:, :]),
                            reads=[xt, identF], writes=[p])
                    fw.op("scalar", lambda h, p=p, b=b, s=s: h.copy(
                        dst_view[:, 4 * b:4 * b + 4, s * 128:(s + 1) * 128],
                        p.t[:, 0:512].rearrange("p (a c) -> p a c", c=128)), reads=[p], writes=[dst_tile])

        def fm_matmul(p, wch, b, ncols=G, rhs_t=None, rhs_tile=None, nk=8, col0=0):
            for kc in range(nk):
                fw.op("tensor", lambda h, kc=kc: h.matmul(
                    p.t[:, col0:col0 + ncols], wch.t[:, (b * nk + kc) * 128:(b * nk + kc + 1) * 128],
                    rhs_t[:, kc, :], start=(kc == 0), stop=(kc == nk - 1)),
                    reads=[wch, rhs_tile], writes=[p])

        def rope_proj(g, c0, dst_t, dst_fn):
            wch = next_chunk(c0)
            for j in range(4):
                p = nextpx()
                fm_matmul(p, wch, j, rhs_t=xT.t, rhs_tile=xT)
                fw.op("vector", lambda h, p=p, j=j: h.tensor_tensor(s4a.t[:, j, :], p.t[:, 0:G], ropeFg.t[:, 0, :], op=ALU.mult),
                      reads=[p, ropeFg], writes=[s4a])
                yield
            wch = next_chunk(c0 + 1)
            for j in range(4):
                p = nextpx()
                fm_matmul(p, wch, j, rhs_t=xT.t, rhs_tile=xT)
                fw.op("vector", lambda h, p=p, j=j: h.tensor_tensor(rtmp.t[:, :], p.t[:, 0:G], ropeFg.t[:, 1, :], op=ALU.mult),
                      reads=[p, ropeFg], writes=[rtmp])
                fw.op("gpsimd", lambda h, j=j: h.tensor_tensor(dst_fn(j), s4a.t[:, j, :], rtmp.t[:, :], op=ALU.add),
                      reads=[s4a, rtmp], writes=[dst_t])
                yield

        def P_a(g):
            xt = xg[g % 2]
            t0 = g * G
            transposes_to(xt, xT.t, xT)
            run(rope_proj(g, 6, qiT, lambda j: qiT.t[:, j, :]))
            wch = next_chunk(9)
            for s in range(NSUB):
                p = nextpx()
                for kc in range(8):
                    fw.op("tensor", lambda h, p=p, kc=kc, s=s, wch=wch: h.matmul(
                        p.t[:, 0:72], xT.t[:, kc, s * 128:(s + 1) * 128], wch.t[:, kc * 72:(kc + 1) * 72],
                        start=(kc == 0), stop=(kc == 7)), reads=[xT, wch], writes=[p])
                fw.op("vector", lambda h, p=p, s=s: h.tensor_scalar(wsc.t[:, s, :], p.t[:, 64:72], float(8.0 ** -0.5 * 0.125), None, op0=ALU.mult),
                      reads=[p], writes=[wsc])
                fw.op("vector", lambda h, p=p: h.bn_stats(stats.t[:, 0:6], p.t[:, 0:64]), reads=[p], writes=[stats])
                fw.op("vector", lambda h: h.bn_aggr(small_ln.t[:, c_mean:c_mean + 2], stats.t[:, 0:6]), reads=[stats], writes=[small_ln])
                fw.op("scalar", lambda h: h.activation(scl(c_rstd), scl(c_var), AF.Sqrt, bias=cv_eps(), scale=1.0),
                      reads=[small_ln, epsT], writes=[small_ln])
                fw.op("vector", lambda h: h.reciprocal(scl(c_rstd), scl(c_rstd)), reads=[small_ln], writes=[small_ln])
                fw.op("vector", lambda h, p=p: h.tensor_scalar(kn.t[:, :], p.t[:, 0:64], scl(c_mean), scl(c_rstd),
                                                               op0=ALU.subtract, op1=ALU.mult), reads=[p, small_ln], writes=[kn])
                fw.op("vector", lambda h: h.tensor_tensor(kn.t[:, :], kn.t[:, :], idxbc.t[:, 0, :], op=ALU.mult), reads=[kn, idxbc], writes=[kn])
                fw.op("vector", lambda h: h.tensor_tensor(kn.t[:, :], kn.t[:, :], idxbc.t[:, 1, :], op=ALU.add), reads=[kn, idxbc], writes=[kn])
                fw.op("vector", lambda h, s=s: h.tensor_tensor(kA.t[:, :], kn.t[:, :], ropeTg.t[:, s, 0:64], op=ALU.mult),
                      reads=[kn, ropeTg], writes=[kA])
                fw.op("vector", lambda h, s=s: h.tensor_tensor(kB.t[:, 0:32], kn.t[:, 32:64], ropeTg.t[:, s, 64:96], op=ALU.mult),
                      reads=[kn, ropeTg], writes=[kB])
                fw.op("vector", lambda h, s=s: h.tensor_tensor(kB.t[:, 32:64], kn.t[:, 0:32], ropeTg.t[:, s, 96:128], op=ALU.mult),
                      reads=[kn, ropeTg], writes=[kB])
                fw.op("vector", lambda h: h.tensor_tensor(kr2.t[:, 0:64], kA.t[:, :], kB.t[:, :], op=ALU.add), reads=[kA, kB], writes=[kr2])
                fw.op("vector", lambda h: h.tensor_tensor(kr2.t[:, 64:128], kA.t[:, :], kB.t[:, :], op=ALU.add), reads=[kA, kB], writes=[kr2])
                p2 = nextpx()
                fw.op("tensor", lambda h, p2=p2: h.transpose(p2.t[:, 0:128], kr2.t[:, :], identF.t[:, :]), reads=[kr2, identF], writes=[p2])
                fw.op("scalar", lambda h, p2=p2, s=s: h.copy(kiT.t[:, t0 + s * 128:t0 + (s + 1) * 128], p2.t[:, 0:128]),
                      reads=[p2], writes=[kiTd[g]])

        def P_b(g):
            xt = xg[g % 2]
            t0 = g * G
            wch = next_chunk(0)
            for j in range(4):
                p = nextpx()
                fm_matmul(p, wch, j, rhs_t=xT.t, rhs_tile=xT)
                fw.op("scalar", lambda h, p=p, j=j: h.activation(s4a.t[:, j, :], p.t[:, 0:G], AF.Sigmoid),
                      reads=[p], writes=[s4a])
                yield
            if g > 0:
                fw.op("gpsimd", lambda h: h.tensor_copy(u.t[:, :, 0:30], u.t[:, :, G:G + 30]), reads=[u], writes=[u])
            wch = next_chunk(1)
            for j in range(4):
                p = nextpx()
                fm_matmul(p, wch, j, rhs_t=xT.t, rhs_tile=xT)
                fw.op("vector", lambda h, p=p, j=j: h.tensor_tensor(u.t[:, j, 30:30 + G], p.t[:, 0:G], s4a.t[:, j, :], op=ALU.mult),
                      reads=[p, s4a], writes=[u])
                yield
            for _ in rope_proj(g, 2, qT, lambda j: qT.t[:, j, :]):
                yield
            for _ in rope_proj(g, 4, KTd[g], lambda j: KT.t[:, j, t0:t0 + G]):
                yield
            wch = next_chunk(8)
            for s in range(NSUB):
                p = nextpx()
                for kc in range(8):
                    fw.op("tensor", lambda h, p=p, kc=kc, s=s, wch=wch: h.matmul(
                        p.t[:, 0:512], xT.t[:, kc, s * 128:(s + 1) * 128], wch.t[:, kc * 512:(kc + 1) * 512],
                        start=(kc == 0), stop=(kc == 7)), reads=[xT, wch], writes=[p])
                kt = g * NSUB + s
                fw.op("scalar", lambda h, p=p, kt=kt: h.copy(
                    Vp.t[:, kt, :, 0:64], p.t[:, 0:512].rearrange("p (a c) -> p a c", c=64)), reads=[p], writes=[Vpd[g]])
                yield
            if g + 1 < NG:
                load_tabs(g + 1)

        def CIA(g):

            def conv_taps():
                for j in range(4):
                    wch = next_chunk(36 + j)
                    pc = nextpx()
                    for i in range(31):
                        fw.op("tensor", lambda h, j=j, i=i, pc=pc, wch=wch: h.matmul(
                            pc.t[:, 0:G], wch.t[:, i * 128:(i + 1) * 128], u.t[:, j, i:i + G], start=(i == 0), stop=(i == 30)),
                            reads=[wch, u], writes=[pc])
                    fw.op("scalar", lambda h, j=j, pc=pc: h.activation(acc.t[:, j, :], pc.t[:, 0:G], AF.Identity, bias=cv(124 + j), scale=1.0),
                          reads=[pc, cvec], writes=[acc])
                    yield

            def conv_ln():
                pst = nextpx()
                for j in range(4):
                    fw.op("scalar", lambda h, j=j: h.activation(s4a.t[:, j, :], acc.t[:, j, :], AF.Square), reads=[acc], writes=[s4a])
                for j in range(4):
                    fw.op("tensor", lambda h, j=j: h.matmul(pst.t[:, 0:G], onesM.t[:, :], acc.t[:, j, :], start=(j == 0), stop=(j == 3)),
                          reads=[onesM, acc], writes=[pst])
                for j in range(4):
                    fw.op("tensor", lambda h, j=j: h.matmul(pst.t[:, G:2 * G], onesM.t[:, :], s4a.t[:, j, :], start=(j == 0), stop=(j == 3)),
                          reads=[onesM, s4a], writes=[pst])
                fw.op("scalar", lambda h: h.copy(cmean.t[:, :], pst.t[:, 0:G]), reads=[pst], writes=[cmean])
                fw.op("vector", lambda h: h.tensor_tensor(crstd.t[:, :], cmean.t[:, :], cmean.t[:, :], op=ALU.mult), reads=[cmean], writes=[crstd])
                fw.op("vector", lambda h: h.tensor_tensor(crstd.t[:, :], pst.t[:, G:2 * G], crstd.t[:, :], op=ALU.subtract),
                      reads=[pst, crstd], writes=[crstd])
                fw.op("scalar", lambda h: h.activation(crstd.t[:, :], crstd.t[:, :], AF.Sqrt, bias=cv_eps(), scale=1.0),
                      reads=[crstd, epsT], writes=[crstd])
                fw.op("vector", lambda h: h.reciprocal(crstd.t[:, :], crstd.t[:, :]), reads=[crstd], writes=[crstd])
                for j in range(4):
                    fw.op("gpsimd", lambda h, j=j: h.tensor_tensor(acc.t[:, j, :], acc.t[:, j, :], cmean.t[:, :], op=ALU.subtract),
                          reads=[acc, cmean], writes=[acc])
                    fw.op("gpsimd", lambda h, j=j: h.tensor_tensor(acc.t[:, j, :], acc.t[:, j, :], crstd.t[:, :], op=ALU.mult),
                          reads=[acc, crstd], writes=[acc])
                    fw.op("scalar", lambda h, j=j: h.activation(convact.t[:, j, :], acc.t[:, j, :], AF.Silu, bias=cv(132 + j), scale=cv(128 + j)),
                          reads=[acc, cvec], writes=[convact])

            def indexer_a(s):
                qt = g * NSUB + s
                n = (qt + 1) * 128
                nchunks = (n + 511) // 512
                it = 0
                for c in range(nchunks):
                    wc = min(512, n - 512 * c)
                    kdeps = [kiTd[gg] for gg in range((512 * c) // G, (512 * c + wc - 1) // G + 1)]
                    for hd in range(8):
                        p = nextpx()
                        rl = relu_ts[it % 2]
                        it += 1
                        r0 = 64 * (hd % 2)
                        fw.op("tensor", lambda h, p=p, hd=hd, r0=r0, c=c, wc=wc: h.matmul(
                            p.t[:, 0:wc], qiT.t[r0:r0 + 64, hd // 2, s * 128:(s + 1) * 128],
                            kiT.t[r0:r0 + 64, 512 * c:512 * c + wc], start=True, stop=True),
                            reads=[qiT] + kdeps, writes=[p])
                        fw.op("scalar", lambda h, p=p, wc=wc, rl=rl: h.activation(rl.t[:, 0:wc], p.t[:, 0:wc], AF.Relu),
                              reads=[p], writes=[rl])
                        if hd == 0:
                            fw.op("vector", lambda h, c=c, wc=wc, rl=rl: h.tensor_scalar(
                                iscore.t[:, 512 * c:512 * c + wc], rl.t[:, 0:wc], wsc.t[:, s, 0:1], None, op0=ALU.mult),
                                reads=[rl, wsc], writes=[iscore])
                        else:
                            fw.op("vector", lambda h, c=c, wc=wc, hd=hd, rl=rl: h.scalar_tensor_tensor(
                                iscore.t[:, 512 * c:512 * c + wc], rl.t[:, 0:wc], wsc.t[:, s, hd:hd + 1],
                                iscore.t[:, 512 * c:512 * c + wc], op0=ALU.mult, op1=ALU.add),
                                reads=[rl, wsc, iscore], writes=[iscore])
                        yield
                fw.op("vector", lambda h: h.memset(iscore.t[0:64, n - 64:n], NEG), writes=[iscore])
                yield

            def bisect(s):
                qt = g * NSUB + s
                n = (qt + 1) * 128
                if n - 64 > topk:
                    fw.op("vector", lambda h: h.tensor_reduce(sc(c_mx), iscore.t[:, 0:n], axis=AX.X, op=ALU.max),
                          reads=[iscore], writes=[small])
                    fw.op("vector", lambda h: h.tensor_reduce(sc(c_lo), iscore.t[:, 0:n - 64], axis=AX.X, op=ALU.min),
                          reads=[iscore], writes=[small])
                    fw.op("vector", lambda h: h.tensor_tensor(sc(c_rng), sc(c_mx), sc(c_lo), op=ALU.subtract), reads=[small], writes=[small])
                    fw.op("vector", lambda h: h.tensor_scalar(steps.t[:, :], cvec.t[:, 152:152 + nb], sc(c_rng), None, op0=ALU.mult),
                          reads=[small, cvec], writes=[steps])
                    nd = max(16, int(round(n * 0.50 / 16.0)) * 16)
                    n_act = n - nd
                    thrc = float(2 * topk - 1 - n_act)
                    ht_ap = half_thr(thrc)
                    nb_t = max(16, nb - int(np.floor(np.log2(4096.0 / n)))) if n < 4096 else nb
                    fw.op("vector", lambda h: h.tensor_tensor(t_mid.t[:, :], sc(c_lo), steps.t[:, 0:1], op=ALU.add),
                          reads=[small, steps], writes=[t_mid])
                    for k in range(nb_t):
                        fw.op("vector", lambda h: h.tensor_scalar(junk.t[:, 0:nd], iscore.t[:, 0:nd], t_mid.t[:, 0:1], None,
                                                                  op0=ALU.is_ge, op1=ALU.add, accum_out=t_cd.t[:, 0:1]),
                              reads=[iscore, t_mid], writes=[junk_lo, t_cd])
                        fw.op("scalar", lambda h: h.activation(junk.t[:, nd:n], iscore.t[:, nd:n], AF.Sign, bias=t_mid.t[:, 0:1], scale=-1.0,
                                                               accum_out=t_sg.t[:, 0:1]),
                              reads=[iscore, t_mid], writes=[junk_hi, t_sg])
                        fw.op("scalar", lambda h: h.activation(t_t.t[:, :], t_sg.t[:, :], AF.Identity, bias=ht_ap, scale=0.5),
                              reads=[t_sg, cvec2], writes=[t_t])
                        fw.op("vector", lambda h, k=k: h.scalar_tensor_tensor(t_d.t[:, :], t_cd.t[:, :], t_t.t[:, 0:1], steps.t[:, k:k + 1],
                                                                               op0=ALU.is_ge, op1=ALU.mult),
                              reads=[t_cd, t_t, steps], writes=[t_d])
                        if k < nb_t - 1:
                            fw.op("vector", lambda h, k=k: h.scalar_tensor_tensor(t_mid.t[:, :], t_d.t[:, :], steps.t[:, k + 1:k + 2], t_mid.t[:, :],
                                                                                   op0=ALU.subtract, op1=ALU.add),
                                  reads=[t_d, steps, t_mid], writes=[t_mid])
                        else:
                            fw.op("vector", lambda h, k=k: h.scalar_tensor_tensor(sc(c_lo), t_d.t[:, :], steps.t[:, k:k + 1], t_mid.t[:, :],
                                                                                   op0=ALU.subtract, op1=ALU.add),
                                  reads=[t_d, steps, t_mid], writes=[small])
                        yield
                else:
                    fw.op("vector", lambda h: h.memset(sc(c_lo), -1.0e29), writes=[small])
                    yield

            def indexer_b(s):
                qt = g * NSUB + s
                mT = maskTd[qt % 2]
                mTbase = (qt % 2) * 4096
                nkt = qt + 1
                for kb in range((nkt + 7) // 8):
                    k0 = kb * 8
                    k1 = min(nkt, k0 + 8)
                    w8 = (k1 - k0) * 128
                    fw.op("vector", lambda h, k0=k0, w8=w8: h.tensor_scalar(mask8.t[:, 0:w8], iscore.t[:, k0 * 128:k0 * 128 + w8],
                                                                            sc(c_lo), -30000.0, op0=ALU.is_lt, op1=ALU.mult),
                          reads=[iscore, small], writes=[mask8])
                    for kk in range(k1 - k0):
                        fw.op("tensor", lambda h, kk=kk: h.transpose(pTb.t[:, kk * 128:(kk + 1) * 128], mask8.t[:, kk * 128:(kk + 1) * 128],
                                                                     identB.t[:, :]), reads=[mask8, identB], writes=[pTb])
                    fw.op("scalar", lambda h, k0=k0, w8=w8: h.copy(big16.t[:, mTbase + k0 * 128:mTbase + k0 * 128 + w8], pTb.t[:, 0:w8]),
                          reads=[pTb], writes=[mT])

            def attention(s):
                qt = g * NSUB + s
                mT = maskTd[qt % 2]
                mTbase = (qt % 2) * 4096
                nkt = qt + 1

                def emit_S(kt):
                    gk = kt // NSUB
                    banks = [pX[2 * (kt % 2)], pX[2 * (kt % 2) + 1]]
                    for hd in range(8):
                        r0 = 64 * (hd % 2)
                        bk = banks[hd % 2]
                        fw.op("tensor", lambda h, hd=hd, r0=r0, bk=bk: h.matmul(
                            bk.t[:, (hd // 2) * 128:(hd // 2 + 1) * 128],
                            KT.t[r0:r0 + 64, hd // 2, kt * 128:(kt + 1) * 128],
                            qT.t[r0:r0 + 64, hd // 2, s * 128:(s + 1) * 128], start=(hd // 2 == 0), stop=False,
                            skip_group_check=True), reads=[KTd[gk], qT], writes=[bk])
                    for hb in range(2):
                        bk = banks[hb]
                        fw.op("tensor", lambda h, bk=bk: h.matmul(
                            bk.t[:, 0:512].rearrange("p (a c) -> p a c", c=128), identB.t[:, :],
                            big16.t[:, mTbase + kt * 128:mTbase + (kt + 1) * 128].unsqueeze(1).broadcast_to([128, 4, 128]),
                            start=False, stop=True, skip_group_check=True), reads=[identB, mT], writes=[bk])

                def emit_exp(kt):
                    banks = [pX[2 * (kt % 2)], pX[2 * (kt % 2) + 1]]
                    pts = [PT[(2 * kt) % 4], PT[(2 * kt + 1) % 4]]
                    for hb in range(2):
                        fw.op("scalar", lambda h, hb=hb: h.activation(pts[hb].t[:, :], banks[hb].t[:, :], AF.Exp, scale=0.125),
                              reads=[banks[hb]], writes=[pts[hb]])

                def emit_pv(kt):
                    gk = kt // NSUB
                    pts = [PT[(2 * kt) % 4], PT[(2 * kt + 1) % 4]]
                    for hd in range(8):
                        ob = pO[hd // 4]
                        fw.op("tensor", lambda h, hd=hd, ob=ob: h.matmul(
                            ob.t[:, (hd % 4) * 65:(hd % 4) * 65 + 65],
                            pts[hd % 2].t[:, (hd // 2) * 128:(hd // 2 + 1) * 128],
                            Vp.t[:, kt, hd, :], start=(kt == 0 and hd % 4 == 0), stop=(kt == nkt - 1),
                            skip_group_check=True), reads=[pts[hd % 2], Vpd[gk]], writes=[ob])

                emit_S(0)
                emit_exp(0)
                for kt in range(nkt):
                    if kt + 1 < nkt:
                        emit_S(kt + 1)
                        emit_exp(kt + 1)
                    emit_pv(kt)
                    yield

            def attention_tail(s):
                for b in range(2):
                    fw.op("vector", lambda h, b=b: h.reciprocal(rden.t[:, 4 * b:4 * b + 4], pO[b].t[:, 64:260:65]), reads=[pO[b]], writes=[rden])
                    fw.op("vector", lambda h, b=b: h.tensor_tensor(
                        attn.t[:, 256 * b:256 * (b + 1)].rearrange("p (a c) -> p a c", c=64),
                        pO[b].t[:, 0:260].rearrange("p (a c) -> p a c", c=65)[:, :, 0:64],
                        rden.t[:, 4 * b:4 * b + 4].unsqueeze(2).broadcast_to([128, 4, 64]), op=ALU.mult),
                        reads=[pO[b], rden], writes=[attn])
                for j in range(4):
                    fw.op("tensor", lambda h, j=j: h.transpose(pTb.t[:, j * 128:(j + 1) * 128], attn.t[:, j * 128:(j + 1) * 128], identB.t[:, :]),
                          reads=[attn, identB], writes=[pTb])
                fw.op("scalar", lambda h: h.copy(attnT.t[:, :, s * 128:(s + 1) * 128],
                                                 pTb.t[:, 0:512].rearrange("p (a c) -> p a c", c=128)), reads=[pTb], writes=[attnT])

            return dict(conv_taps=conv_taps, conv_ln=conv_ln, indexer_a=indexer_a, bisect=bisect, indexer_b=indexer_b, attention=attention,
                        attention_tail=attention_tail)

        def M_step(g):
            xt = xg[g % 2]
            for j in range(8):
                wM = next_chunk(10 + j)
                pg = nextpx()
                fm_matmul(pg, wM, 0, rhs_t=xT.t, rhs_tile=xT, col0=0)
                fm_matmul(pg, wM, 1, rhs_t=xT.t, rhs_tile=xT, col0=G)
                po = nextpx()
                fm_matmul(po, wM, 4, rhs_t=convact.t, rhs_tile=convact, nk=4, col0=0)
                fm_matmul(po, wM, 5, rhs_t=attnT.t, rhs_tile=attnT, nk=4, col0=G)
                fw.op("scalar", lambda h, pg=pg, j=j: h.activation(s4a.t[:, 0, :], pg.t[:, 0:G], AF.Sigmoid, bias=cv(136 + j), scale=1.0),
                      reads=[pg, cvec], writes=[s4a])
                fw.op("scalar", lambda h, pg=pg, j=j: h.activation(s4a.t[:, 1, :], pg.t[:, G:2 * G], AF.Sigmoid, bias=cv(144 + j), scale=1.0),
                      reads=[pg, cvec], writes=[s4a])
                fw.op("vector", lambda h, po=po: h.tensor_tensor(s4a.t[:, 2, :], po.t[:, 0:G], s4a.t[:, 0, :], op=ALU.mult),
                      reads=[po, s4a], writes=[s4a])
                fw.op("vector", lambda h, po=po: h.tensor_tensor(s4a.t[:, 3, :], po.t[:, G:2 * G], s4a.t[:, 1, :], op=ALU.mult),
                      reads=[po, s4a], writes=[s4a])
                fw.op("gpsimd", lambda h, j=j: h.tensor_tensor(mergedT.t[:, j, :], s4a.t[:, 2, :], s4a.t[:, 3, :], op=ALU.add),
                      reads=[s4a], writes=[mergedT])
            for hh in range(2):
                wch = next_chunk(18 + hh)
                for s in range(NSUB):
                    p = nextpx()
                    for kc in range(8):
                        fw.op("tensor", lambda h, p=p, kc=kc, s=s, wch=wch: h.matmul(
                            p.t[:, 0:512], mergedT.t[:, kc, s * 128:(s + 1) * 128], wch.t[:, kc * 512:(kc + 1) * 512],
                            start=(kc == 0), stop=(kc == 7)), reads=[mergedT, wch], writes=[p])
                    fw.op("vector", lambda h, p=p, s=s, hh=hh: h.scalar_tensor_tensor(
                        xt.t[:, s, hh * 512:(hh + 1) * 512], xt.t[:, s, hh * 512:(hh + 1) * 512], float(ALPHA), p.t[:, 0:512],
                        op0=ALU.mult, op1=ALU.add), reads=[xt, p], writes=[xt])
            for s in range(NSUB):
                layer_norm_rows(xt, s, 0)
            load_ln(1)

        def F_step(g):
            xt = xg[g % 2]
            t0 = g * G
            transposes_to(xt, x1T_view, s4a)
            yield
            hdeps = maskTd
            for c in range(8):
                wch = next_chunk(20 + c)
                for b in range(4):
                    blk = 4 * c + b
                    p = nextpx()
                    fm_matmul(p, wch, b, rhs_t=x1T_view, rhs_tile=s4a)
                    fw.op("scalar", lambda h, p=p: h.activation(rtmp.t[:, 0:G], p.t[:, 0:G], AF.Relu), reads=[p], writes=[rtmp])
                    fw.op("gpsimd", lambda h, blk=blk: h.tensor_tensor(big16.t[:, blk * G:(blk + 1) * G], rtmp.t[:, 0:G], rtmp.t[:, 0:G], op=ALU.mult),
                          reads=[rtmp], writes=hdeps)
                    yield
            for hh in range(2):
                for c in range(4):
                    wch = next_chunk(28 + 4 * hh + c)
                    for s in range(NSUB):
                        for jj in range(8):
                            j = 8 * c + jj
                            fw.op("tensor", lambda h, s=s, j=j, jj=jj, wch=wch: h.matmul(
                                pO[s].t[:, 0:512], big16.t[:, j * G + s * 128:j * G + (s + 1) * 128], wch.t[:, jj * 512:(jj + 1) * 512],
                                start=(j == 0), stop=(j == 31)), reads=hdeps + [wch], writes=[pO[s]])
                        yield
                for s in range(NSUB):
                    fw.op("vector", lambda h, s=s, hh=hh: h.scalar_tensor_tensor(
                        xt.t[:, s, hh * 512:(hh + 1) * 512], xt.t[:, s, hh * 512:(hh + 1) * 512], float(ALPHA), pO[s].t[:, 0:512],
                        op0=ALU.mult, op1=ALU.add), reads=[xt, pO[s]], writes=[xt])
                yield
            for s in range(NSUB):
                layer_norm_rows(xt, s, 1)
                yield
            if g + 1 < NG:
                load_ln(0)
            out_waits.append(fw.dma("gpsimd", lambda: out_d[t0:t0 + G, :].rearrange("(s p) d -> p s d", p=128),
                                    lambda: xt.t[:, :, :], reads=[xt], pe=out_pes[g % 2]))

        def run(gen):
            for _ in gen:
                pass

        def interleave(ga, gb):
            da = db = False
            while not (da and db):
                if not da:
                    try:
                        next(ga)
                    except StopIteration:
                        da = True
                if not db:
                    try:
                        next(gb)
                    except StopIteration:
                        db = True

        def chain(*gens):
            for gg_ in gens:
                for _ in gg_:
                    yield

        def interleave_until(ga, gb, na=1, nb=1):
            while True:
                for _ in range(na):
                    try:
                        next(ga)
                    except StopIteration:
                        return
                for _ in range(nb):
                    try:
                        next(gb)
                    except StopIteration:
                        break

        def CIA_first(g, fns, filler):
            interleave_until(fns["indexer_a"](0), filler, na=3, nb=1)
            interleave_until(fns["bisect"](0), filler, na=1, nb=2)
            run(filler)

        def CIA_rest(g, fns):
            fns["indexer_b"](0)
            filler = chain(fns["attention"](0), fns["conv_taps"]())
            interleave_until(fns["indexer_a"](1), filler, na=4, nb=1)
            interleave_until(fns["bisect"](1), filler, na=1, nb=2)
            run(filler)
            fns["attention_tail"](0)
            fns["conv_ln"]()
            fns["indexer_b"](1)
            run(fns["attention"](1))
            fns["attention_tail"](1)

        assert NSUB == 2
        load_x(0)
        load_tabs(0)
        load_ln(0)
        P_a(0)
        fns = CIA(0)
        CIA_first(0, fns, P_b(0))
        CIA_rest(0, fns)
        for g in range(NG):
            if g + 1 < NG:
                load_x(g + 1)
            M_step(g)
            if g + 1 < NG:
                P_a(g + 1)
                fns = CIA(g + 1)
                CIA_first(g + 1, fns, chain(F_step(g), P_b(g + 1)))
                CIA_rest(g + 1, fns)
            else:
                run(F_step(g))
        fw.finish(out_waits)
    return nc


_PROG_CACHE = {}


def run_cores(xs, consts, wall, L, topk):
    key = (L, topk)
    if key not in _PROG_CACHE:
        _PROG_CACHE[key] = build_program(L, topk)
    nc = _PROG_CACHE[key]
    in_maps = []
    for xb in xs:
        m = {"x": np.ascontiguousarray(xb, dtype=np.float32), "wall": wall}
        m.update(consts)
        in_maps.append(m)
    res = run_bass_kernel_spmd(nc, in_maps, core_ids=list(range(len(xs))))
    return [r["out"] for r in res.results]


def kernel(**inputs):
    x = np.asarray(inputs["x"], np.float32)
    B, L, _ = x.shape
    topk = min(256, L // 4)
    wall = host_weights(inputs)
    consts = host_consts(inputs, L)
    outs = run_cores([x[b] for b in range(B)], consts, wall, L, topk)
    return np.stack(outs, axis=0).astype(np.float32)
```

```python
import numpy as np
from contextlib import ExitStack
import concourse.bass as bass
import concourse.mybir as mybir
from concourse.bass_utils import run_bass_kernel_spmd

F32 = mybir.dt.float32
BF16 = mybir.dt.bfloat16
U8 = mybir.dt.uint8
AF = mybir.ActivationFunctionType
ALU = mybir.AluOpType
AX = mybir.AxisListType

D = 1024
NB_BISECT = 22
G = 256
NSUB = G // 128
CHW = 4096
NSLOT = 3
ALPHA = 2.0 ** 0.25
LN_EPS = 1e-5
NEG = -1.0e30
N_CHUNK = 40


class Eng:
    def __init__(self, name, sem, same_sync):
        self.name = name
        self.sem = sem
        self.count = 0
        self.waited = {}
        self.ops = []
        self.same_sync = same_sync


class Rec:
    def __init__(self):
        self.call = None

    def __getattr__(self, name):
        def f(*a, **k):
            self.call = (name, a, k)
            return self
        return f


class Dep:
    def __init__(self, name="dep"):
        self.name = name
        self.last_write = None
        self.reads = []
        self.dma_eng = None


class Tile(Dep):
    def __init__(self, name, t, excl=False):
        Dep.__init__(self, name)
        self.t = t
        self.excl = excl

    def __getitem__(self, idx):
        return self.t[idx]


class FW:
    def __init__(self, nc, stack):
        self.nc = nc
        self.stack = stack
        self.engs = {}
        self.muted = False
        for n in ["tensor", "vector", "scalar", "gpsimd", "sync"]:
            sem = stack.enter_context(nc.semaphore("sem_" + n))
            self.engs[n] = Eng(n, sem, same_sync=(n in ("vector", "scalar", "gpsimd")))

    def sb(self, name, shape, dt):
        return Tile(name, self.stack.enter_context(self.nc.sbuf_tensor("sb_" + name, shape, dt)))

    def ps(self, name, shape, dt):
        return Tile(name, self.stack.enter_context(self.nc.psum_tensor("ps_" + name, shape, dt)), excl=True)

    def dma_eng(self, name):
        sem = self.stack.enter_context(self.nc.semaphore("dsem_" + name))
        return Eng("dma_" + name, sem, False)

    def _deps(self, eng, reads, writes):
        deps = []
        for t in reads:
            if t.last_write is not None:
                deps.append(t.last_write)
        for t in writes:
            if t.last_write is not None:
                deps.append(t.last_write)
            deps.extend(t.reads)
        waits = {}
        for (e2, seq) in deps:
            if e2 is eng and not eng.same_sync:
                continue
            if eng.waited.get(e2, 0) >= seq:
                continue
            if waits.get(e2, 0) < seq:
                waits[e2] = seq
        for e2, seq in waits.items():
            eng.waited[e2] = seq
        return list(waits.items())

    def _mark(self, who, seq, reads, writes):
        for t in reads:
            t.reads.append((who, seq))
        for t in writes:
            t.last_write = (who, seq)
            t.reads = []

    def op(self, engname, fn, reads=(), writes=()):
        if self.muted:
            return
        eng = self.engs[engname]
        ex = [t for t in reads if getattr(t, "excl", False)]
        if ex:
            writes = list(writes) + [t for t in ex if t not in writes]
        waits = self._deps(eng, reads, writes)
        eng.count += 1
        seq = eng.count
        rec = Rec()
        fn(rec)
        mname, margs, mkw = rec.call

        def thunk(h, waits=waits, eng=eng, mname=mname, margs=margs, mkw=mkw):
            for e2, s in waits:
                h.wait_ge(e2.sem, s)
            getattr(h, mname)(*margs, **mkw).then_inc(eng.sem, 1)

        eng.ops.append(thunk)
        self._mark(eng, seq, reads, writes)

    def dma(self, qname, out_fn, in_fn, reads=(), writes=(), pe=None):
        if self.muted:
            return None
        q = self.engs[qname]
        waits = self._deps(q, reads, writes)
        if pe is None:
            key = writes[0] if writes else reads[0]
            if key.dma_eng is None:
                key.dma_eng = self.dma_eng(key.name)
            pe = key.dma_eng
        pe.count += 16
        seq = pe.count
        o_ap = out_fn()
        i_ap = in_fn()

        def thunk(h, waits=waits, pe=pe, o_ap=o_ap, i_ap=i_ap):
            for e2, s in waits:
                h.wait_ge(e2.sem, s)
            h.dma_start(out=o_ap, in_=i_ap).then_inc(pe.sem, 16)

        q.ops.append(thunk)
        self._mark(pe, seq, reads, writes)
        return pe, seq

    def finish(self, final_waits):
        nc = self.nc
        engs = self.engs
        with nc.Block() as block:
            def mk(name):
                eng = engs[name]

                def body(h):
                    for th in eng.ops:
                        th(h)
                    if name == "sync":
                        for pe, s in final_waits:
                            h.wait_ge(pe.sem, s)
                return body
            block.tensor(mk("tensor"))
            block.vector(mk("vector"))
            block.scalar(mk("scalar"))
            block.gpsimd(mk("gpsimd"))
            block.sync(mk("sync"))


def _fm_block(w, cols):
    K = w.shape[0]
    blk = w[:, cols]
    return blk.reshape(K // 128, 128, 128).transpose(1, 0, 2).reshape(128, -1)


def _kc_major(w):
    K, N = w.shape
    return w.reshape(K // 128, 128, N).transpose(1, 0, 2).reshape(128, -1)


def host_weights(inp):
    w_in = np.asarray(inp["w_in"][0], np.float32)
    oa, ob, oq, ok, ov, oqi, oki, owi, ogc, oga = 0, 512, 1024, 1536, 2048, 2560, 3072, 3136, 3144, 4168
    sw = np.arange(512).reshape(8, 64)
    sw = np.concatenate([sw[:, 32:], sw[:, :32]], axis=1).ravel()
    chunks = []

    def pad(a):
        out = np.zeros((128, CHW), np.float32)
        out[:, :a.shape[1]] = a
        return out

    def fm4(cols512):
        return np.concatenate([_fm_block(w_in, cols512[b * 128:(b + 1) * 128]) for b in range(4)], axis=1)

    r = np.arange(512)
    chunks.append(fm4(ob + r))
    chunks.append(fm4(oa + r))
    chunks.append(fm4(oq + r))
    chunks.append(fm4(oq + sw))
    chunks.append(fm4(ok + r))
    chunks.append(fm4(ok + sw))
    chunks.append(fm4(oqi + r))
    chunks.append(fm4(oqi + sw))
    chunks.append(_kc_major(w_in[:, ov:ov + 512]))
    chunks.append(pad(_kc_major(w_in[:, oki:oki + 72])))
    w_co = np.asarray(inp["w_conv_out"][0], np.float32)
    w_ao = np.asarray(inp["w_attn_out"][0], np.float32)
    r128 = np.arange(128)
    for j in range(8):
        blks = [_fm_block(w_in, ogc + j * 128 + r128), _fm_block(w_in, oga + j * 128 + r128),
                _fm_block(w_co, j * 128 + r128), _fm_block(w_ao, j * 128 + r128)]
        chunks.append(pad(np.concatenate(blks, axis=1)))
    w_out = np.asarray(inp["w_out"][0], np.float32)
    for h in range(2):
        chunks.append(_kc_major(w_out[:, h * 512:(h + 1) * 512]))
    w_fi = np.asarray(inp["w_ff_in"][0], np.float32)
    for c in range(8):
        chunks.append(np.concatenate([_fm_block(w_fi, (4 * c + b) * 128 + r128) for b in range(4)], axis=1))
    w_fo = np.asarray(inp["w_ff_out"][0], np.float32)
    for h in range(2):
        for c in range(4):
            chunks.append(_kc_major(w_fo[c * 1024:(c + 1) * 1024, h * 512:(h + 1) * 512]))
    dw_w = np.asarray(inp["dw_w"][0], np.float32)
    for j in range(4):
        dg = np.zeros((128, 31, 128), np.float32)
        idx = np.arange(128)
        dg[idx, :, idx] = dw_w[:, j * 128:(j + 1) * 128].T
        chunks.append(pad(dg.reshape(128, 31 * 128)))
    wall = np.stack(chunks, axis=0)
    assert wall.shape == (N_CHUNK, 128, CHW), wall.shape
    return np.ascontiguousarray(wall)


def host_consts(inp, L):
    nb = NB_BISECT
    cvec = np.zeros((128, 152 + nb), np.float32)
    dw_w = np.asarray(inp["dw_w"][0], np.float32)
    for j in range(4):
        cvec[:, j * 31:(j + 1) * 31] = dw_w[:, j * 128:(j + 1) * 128].T
    cvec[:, 124:128] = np.asarray(inp["dw_b"][0]).reshape(4, 128).T
    cvec[:, 128:132] = np.asarray(inp["conv_ln_g"][0]).reshape(4, 128).T
    cvec[:, 132:136] = np.asarray(inp["conv_ln_b"][0]).reshape(4, 128).T
    gb = np.asarray(inp["gate_b"][0], np.float32)
    cvec[:, 136:144] = gb[0].reshape(8, 128).T
    cvec[:, 144:152] = gb[1].reshape(8, 128).T
    cvec[:, 152:152 + nb] = (0.5 ** np.arange(1, nb + 1))[None, :]
    lnbc = np.stack([np.broadcast_to(np.asarray(inp[k][0], np.float32)[None, :], (128, D))
                     for k in ("ln1_g", "ln1_b", "ln2_g", "ln2_b")], axis=1)
    idxbc = np.stack([np.broadcast_to(np.asarray(inp[k][0], np.float32)[None, :], (128, 64))
                      for k in ("idx_k_ln_g", "idx_k_ln_b")], axis=1)
    inv_freq = (10000.0 ** (-np.arange(0, 64, 2, dtype=np.float32) / np.float32(64))).astype(np.float32)
    ang = (np.arange(L, dtype=np.float32)[:, None] * inv_freq[None, :]).astype(np.float32)
    cos = np.cos(ang.astype(np.float64)).astype(np.float32)
    sin = np.sin(ang.astype(np.float64)).astype(np.float32)
    cos2 = np.concatenate([cos, cos], axis=1)
    sinS = np.concatenate([-sin, sin], axis=1)
    ropeT = np.concatenate([cos2, sinS], axis=1)
    ropeF = np.stack([np.concatenate([cos2.T, cos2.T], axis=0),
                      np.concatenate([sinS.T, sinS.T], axis=0)], axis=1)
    identF = np.eye(128, dtype=np.float32)
    return dict(cvec=cvec, lnbc=np.ascontiguousarray(lnbc), idxbc=np.ascontiguousarray(idxbc),
                ropeT=np.ascontiguousarray(ropeT), ropeF=np.ascontiguousarray(ropeF), identF=identF)


class _Stop(Exception):
    pass


def build_program(L, topk):
    import os
    STOP = os.environ.get('K_STOP', '')
    NGRUN = int(os.environ.get('K_NG', '0'))
    STOPS = int(os.environ.get('K_STOPS', '0'))
    NG = L // G
    NT = L // 128
    nb = NB_BISECT
    NV = 152 + nb
    nc = bass.Bass("TRN2", target_bir_lowering=False)
    x_d = nc.dram_tensor("x", [L, D], F32, kind="ExternalInput").ap()
    wall_d = nc.dram_tensor("wall", [N_CHUNK, 128, CHW], F32, kind="ExternalInput").ap()
    cvec_d = nc.dram_tensor("cvec", [128, NV], F32, kind="ExternalInput").ap()
    lnbc_d = nc.dram_tensor("lnbc", [128, 4, D], F32, kind="ExternalInput").ap()
    idxbc_d = nc.dram_tensor("idxbc", [128, 2, 64], F32, kind="ExternalInput").ap()
    ropeT_d = nc.dram_tensor("ropeT", [L, 128], F32, kind="ExternalInput").ap()
    ropeF_d = nc.dram_tensor("ropeF", [128, 2, L], F32, kind="ExternalInput").ap()
    identF_d = nc.dram_tensor("identF", [128, 128], F32, kind="ExternalInput").ap()
    out_d = nc.dram_tensor("out", [L, D], F32, kind="ExternalOutput").ap()
    wscr_d = nc.dram_tensor("wscr", [N_CHUNK, 128, CHW], BF16).ap()

    with ExitStack() as st:
        fw = FW(nc, st)
        KT = fw.sb("KT", [128, 4, L], BF16)
        Vp = fw.sb("Vp", [128, NT, 8, 65], BF16)
        kiT = fw.sb("kiT", [128, L], BF16)
        KTd = [Dep("KT%d" % g) for g in range(NG)]
        Vpd = [Dep("Vp%d" % g) for g in range(NG)]
        kiTd = [Dep("kiT%d" % g) for g in range(NG)]
        xg = [fw.sb("xg%d" % i, [128, NSUB, D], F32) for i in range(2)]
        xT = fw.sb("xT", [128, 8, G], BF16)
        s4a = fw.sb("s4a", [128, 4, G], F32)
        u = fw.sb("u", [128, 4, 30 + G], BF16)
        acc = fw.sb("acc", [128, 4, G], F32)
        convact = fw.sb("convact", [128, 4, G], BF16)
        rtmp = fw.sb("rtmp", [128, G], F32)
        rtmp2 = fw.sb("rtmp2", [128, G], F32)
        qT = fw.sb("qT", [128, 4, G], BF16)
        qiT = fw.sb("qiT", [128, 4, G], BF16)
        ropeFg = fw.sb("ropeFg", [128, 2, G], F32)
        ropeTg = fw.sb("ropeTg", [128, NSUB, 128], F32)
        iscore = fw.sb("iscore", [128, max(L, CHW)], F32)
        junk = fw.sb("junk", [128, L], U8)
        mask8 = fw.sb("mask8", [128, 1024], BF16)
        big16 = fw.sb("big16", [128, 8192], BF16)
        maskTd = [Dep("maskT0"), Dep("maskT1")]
        relu_ts = [fw.sb("relu_t%d" % i, [128, 512], F32) for i in range(2)]
        relu_t = relu_ts[0]
        PT = [fw.sb("PT%d" % i, [128, 512], BF16) for i in range(4)]
        attn = fw.sb("attn", [128, 512], BF16)
        attnT = fw.sb("attnT", [128, 4, G], BF16)
        mergedT = fw.sb("mergedT", [128, 8, G], BF16)
        lnbc = fw.sb("lnbc", [128, 2, D], F32)
        idxbc = fw.sb("idxbc", [128, 2, 64], F32)
        cvec = fw.sb("cvec", [128, NV], F32)
        identF = fw.sb("identF", [128, 128], F32)
        identB = fw.sb("identB", [128, 128], BF16)
        onesM = fw.sb("onesM", [128, 128], F32)
        kn = fw.sb("kn", [128, 64], F32)
        kA = fw.sb("kA", [128, 64], F32)
        kB = fw.sb("kB", [128, 64], F32)
        kr2 = fw.sb("kr2", [128, 128], F32)
        wsc = fw.sb("wsc", [128, NSUB, 8], F32)
        small = fw.sb("small", [128, 64], F32)
        steps = fw.sb("steps", [128, nb], F32)
        small_ln = fw.sb("small_ln", [128, 8], F32)
        t_mid = fw.sb("t_mid", [128, 1], F32)
        t_cd = fw.sb("t_cd", [128, 1], F32)
        t_sg = fw.sb("t_sg", [128, 1], F32)
        t_t = fw.sb("t_t", [128, 1], F32)
        t_d = fw.sb("t_d", [128, 1], F32)
        cvec2 = fw.sb("cvec2", [128, 2 * (L // 128) + 4], F32)
        _ht = {}

        def half_thr(v):
            if v not in _ht:
                col = len(_ht)
                fw.op("gpsimd", lambda h: h.memset(cvec2.t[:, col:col + 1], v * 0.5), writes=[cvec2])
                _ht[v] = col
            c_ = _ht[v]
            return cvec2.t[:, c_:c_ + 1]

        junk_lo = Dep("junk_lo")
        junk_hi = Dep("junk_hi")
        stats = fw.sb("stats", [128, 12], F32)
        rden = fw.sb("rden", [128, 8], F32)
        cmean = fw.sb("cmean", [128, G], F32)
        crstd = fw.sb("crstd", [128, G], F32)
        slots = [fw.sb("wslot%d" % i, [128, CHW], BF16) for i in range(NSLOT)]
        NPX = 5
        pX = [fw.ps("pX%d" % i, [128, 512], F32) for i in range(NPX)]
        pO = [fw.ps("pO%d" % i, [128, 512], F32) for i in range(2)]
        pTb = fw.ps("pTb", [128, 1024], BF16)
        pxi = [0]

        def nextpx():
            p = pX[pxi[0] % NPX]
            pxi[0] += 1
            return p

        SM = small.t
        c_mean, c_var, c_rstd, c_lo, c_mid, c_cnt, c_d, c_mx, c_mn, c_rng, c_thr = range(11)

        def sc(i):
            return SM[:, i:i + 1]

        def scl(i):
            return small_ln.t[:, i:i + 1]

        def cv(i):
            return cvec.t[:, i:i + 1]

        fw.dma("sync", lambda: cvec.t[:, :], lambda: cvec_d[:, :], writes=[cvec])
        fw.dma("sync", lambda: identF.t[:, :], lambda: identF_d[:, :], writes=[identF])
        fw.dma("sync", lambda: idxbc.t[:, :, :], lambda: idxbc_d[:, :, :], writes=[idxbc])
        fw.op("vector", lambda h: h.tensor_copy(identB.t[:, :], identF.t[:, :]), reads=[identF], writes=[identB])
        fw.op("gpsimd", lambda h: h.memset(onesM.t[:, :], 1.0 / 512.0), writes=[onesM])
        fw.op("gpsimd", lambda h: h.memset(u.t[:, :, :], 0.0), writes=[u])
        fw.op("gpsimd", lambda h: h.memset(Vp.t[:, :, :, :].rearrange("p a b c -> p (a b) c")[:, :, 64:65], 1.0), writes=Vpd)

        scr_chunk = [Dep("wscr%d" % i) for i in range(N_CHUNK)]
        conv_order = [6, 7, 9, 0, 1, 2, 3, 4, 5, 8] + list(range(36, 40)) + list(range(10, 36))
        assert sorted(conv_order) == list(range(N_CHUNK))
        for i in conv_order:
            fw.dma("gpsimd", lambda i=i: wscr_d[i, :, :], lambda i=i: wall_d[i, :, :], writes=[scr_chunk[i]],
                   pe=fw.dma_eng("wconv%d" % i))

        PA_ORDER = [6, 7, 9]
        PB_ORDER = [0, 1, 2, 3, 4, 5, 8]
        order = PA_ORDER + PB_ORDER + list(range(36, 40))
        for gg in range(NG):
            order += list(range(10, 20))
            if gg + 1 < NG:
                order += PA_ORDER
            order += list(range(20, 36))
            if gg + 1 < NG:
                order += PB_ORDER + list(range(36, 40))
        stream = {"issued": 0, "next": 0}
        total_chunks = len(order)

        def issue_upto(n):
            while stream["issued"] < min(n, total_chunks):
                k = stream["issued"]
                sl = slots[k % NSLOT]
                fw.dma("sync", lambda sl=sl: sl.t[:, :], lambda k=k: wscr_d[order[k], :, :], reads=[scr_chunk[order[k]]], writes=[sl])
                stream["issued"] += 1

        def next_chunk(expect):
            k = stream["next"]
            assert order[k] == expect, (k, order[k], expect)
            issue_upto(k + NSLOT)
            stream["next"] += 1
            return slots[k % NSLOT]

        out_waits = []
        out_pes = [fw.dma_eng("outst%d" % i) for i in range(2)]

        def load_x(g):
            xt = xg[g % 2]
            fw.dma("sync", lambda: xt.t[:, :, :],
                   lambda: x_d[g * G:(g + 1) * G, :].rearrange("(s p) d -> p s d", p=128), writes=[xt])

        def load_tabs(g):
            fw.dma("sync", lambda: ropeFg.t[:, :, :], lambda: ropeF_d[:, :, g * G:(g + 1) * G], writes=[ropeFg])
            fw.dma("sync", lambda: ropeTg.t[:, :, :],
                   lambda: ropeT_d[g * G:(g + 1) * G, :].rearrange("(s p) d -> p s d", p=128), writes=[ropeTg])

        def load_ln(which):
            fw.dma("sync", lambda: lnbc.t[:, :, :], lambda: lnbc_d[:, 2 * which:2 * which + 2, :], writes=[lnbc])

        def layer_norm_rows(xt, s, which_loaded):
            X = xt.t
            for hh in range(2):
                fw.op("vector", lambda h, hh=hh: h.bn_stats(stats.t[:, hh * 6:(hh + 1) * 6], X[:, s, hh * 512:(hh + 1) * 512]),
                      reads=[xt], writes=[stats])
            fw.op("vector", lambda h: h.bn_aggr(small_ln.t[:, c_mean:c_mean + 2], stats.t[:, 0:12]), reads=[stats], writes=[small_ln])
            fw.op("scalar", lambda h: h.activation(scl(c_rstd), scl(c_var), AF.Sqrt, bias=cv_eps(), scale=1.0),
                  reads=[small_ln, epsT], writes=[small_ln])
            fw.op("vector", lambda h: h.reciprocal(scl(c_rstd), scl(c_rstd)), reads=[small_ln], writes=[small_ln])
            fw.op("vector", lambda h: h.tensor_scalar(X[:, s, :], X[:, s, :], scl(c_mean), scl(c_rstd), op0=ALU.subtract, op1=ALU.mult),
                  reads=[xt, small_ln], writes=[xt])
            fw.op("gpsimd", lambda h: h.tensor_tensor(X[:, s, :], X[:, s, :], lnbc.t[:, 0, :], op=ALU.mult), reads=[xt, lnbc], writes=[xt])
            fw.op("gpsimd", lambda h: h.tensor_tensor(X[:, s, :], X[:, s, :], lnbc.t[:, 1, :], op=ALU.add), reads=[xt, lnbc], writes=[xt])

        epsT = fw.sb("epsT", [128, 1], F32)
        fw.op("gpsimd", lambda h: h.memset(epsT.t[:, :], LN_EPS), writes=[epsT])

        def cv_eps():
            return epsT.t[:, 0:1]

        x1T_view = s4a.t[:, :, :].rearrange("p a b -> p (a b)").bitcast(BF16).rearrange("p (k t) -> p k t", t=G)

        def transposes_to(xt, dst_view, dst_tile):
            for s in range(NSUB):
                for b in range(2):
                    p = nextpx()
                    for kk in range(4):
                        kc = 4 * b + kk
                        fw.op("tensor", lambda h, p=p, kk=kk, kc=kc, s=s: h.transpose(
                            p.t[:, kk * 128:(kk + 1) * 128], xt.t[:, s, kc * 128:(kc + 1) * 128], identF.t[:, :]),
                            reads=[xt, identF], writes=[p])
                    fw.op("scalar", lambda h, p=p, b=b, s=s: h.copy(
                        dst_view[:, 4 * b:4 * b + 4, s * 128:(s + 1) * 128],
                        p.t[:, 0:512].rearrange("p (a c) -> p a c", c=128)), reads=[p], writes=[dst_tile])

        def fm_matmul(p, wch, b, ncols=G, rhs_t=None, rhs_tile=None, nk=8, col0=0):
            for kc in range(nk):
                fw.op("tensor", lambda h, kc=kc: h.matmul(
                    p.t[:, col0:col0 + ncols], wch.t[:, (b * nk + kc) * 128:(b * nk + kc + 1) * 128],
                    rhs_t[:, kc, :], start=(kc == 0), stop=(kc == nk - 1)),
                    reads=[wch, rhs_tile], writes=[p])

        def rope_proj(g, c0, dst_t, dst_fn):
            wch = next_chunk(c0)
            for j in range(4):
                p = nextpx()
                fm_matmul(p, wch, j, rhs_t=xT.t, rhs_tile=xT)
                fw.op("vector", lambda h, p=p, j=j: h.tensor_tensor(s4a.t[:, j, :], p.t[:, 0:G], ropeFg.t[:, 0, :], op=ALU.mult),
                      reads=[p, ropeFg], writes=[s4a])
                yield
            wch = next_chunk(c0 + 1)
            for j in range(4):
                p = nextpx()
                fm_matmul(p, wch, j, rhs_t=xT.t, rhs_tile=xT)
                fw.op("vector", lambda h, p=p, j=j: h.tensor_tensor(rtmp.t[:, :], p.t[:, 0:G], ropeFg.t[:, 1, :], op=ALU.mult),
                      reads=[p, ropeFg], writes=[rtmp])
                fw.op("gpsimd", lambda h, j=j: h.tensor_tensor(dst_fn(j), s4a.t[:, j, :], rtmp.t[:, :], op=ALU.add),
                      reads=[s4a, rtmp], writes=[dst_t])
                yield

        def P_a(g):
            xt = xg[g % 2]
            t0 = g * G
            transposes_to(xt, xT.t, xT)
            run(rope_proj(g, 6, qiT, lambda j: qiT.t[:, j, :]))
            wch = next_chunk(9)
            for s in range(NSUB):
                p = nextpx()
                for kc in range(8):
                    fw.op("tensor", lambda h, p=p, kc=kc, s=s, wch=wch: h.matmul(
                        p.t[:, 0:72], xT.t[:, kc, s * 128:(s + 1) * 128], wch.t[:, kc * 72:(kc + 1) * 72],
                        start=(kc == 0), stop=(kc == 7)), reads=[xT, wch], writes=[p])
                fw.op("vector", lambda h, p=p, s=s: h.tensor_scalar(wsc.t[:, s, :], p.t[:, 64:72], float(8.0 ** -0.5 * 0.125), None, op0=ALU.mult),
                      reads=[p], writes=[wsc])
                fw.op("vector", lambda h, p=p: h.bn_stats(stats.t[:, 0:6], p.t[:, 0:64]), reads=[p], writes=[stats])
                fw.op("vector", lambda h: h.bn_aggr(small_ln.t[:, c_mean:c_mean + 2], stats.t[:, 0:6]), reads=[stats], writes=[small_ln])
                fw.op("scalar", lambda h: h.activation(scl(c_rstd), scl(c_var), AF.Sqrt, bias=cv_eps(), scale=1.0),
                      reads=[small_ln, epsT], writes=[small_ln])
                fw.op("vector", lambda h: h.reciprocal(scl(c_rstd), scl(c_rstd)), reads=[small_ln], writes=[small_ln])
                fw.op("vector", lambda h, p=p: h.tensor_scalar(kn.t[:, :], p.t[:, 0:64], scl(c_mean), scl(c_rstd),
                                                               op0=ALU.subtract, op1=ALU.mult), reads=[p, small_ln], writes=[kn])
                fw.op("vector", lambda h: h.tensor_tensor(kn.t[:, :], kn.t[:, :], idxbc.t[:, 0, :], op=ALU.mult), reads=[kn, idxbc], writes=[kn])
                fw.op("vector", lambda h: h.tensor_tensor(kn.t[:, :], kn.t[:, :], idxbc.t[:, 1, :], op=ALU.add), reads=[kn, idxbc], writes=[kn])
                fw.op("vector", lambda h, s=s: h.tensor_tensor(kA.t[:, :], kn.t[:, :], ropeTg.t[:, s, 0:64], op=ALU.mult),
                      reads=[kn, ropeTg], writes=[kA])
                fw.op("vector", lambda h, s=s: h.tensor_tensor(kB.t[:, 0:32], kn.t[:, 32:64], ropeTg.t[:, s, 64:96], op=ALU.mult),
                      reads=[kn, ropeTg], writes=[kB])
                fw.op("vector", lambda h, s=s: h.tensor_tensor(kB.t[:, 32:64], kn.t[:, 0:32], ropeTg.t[:, s, 96:128], op=ALU.mult),
                      reads=[kn, ropeTg], writes=[kB])
                fw.op("vector", lambda h: h.tensor_tensor(kr2.t[:, 0:64], kA.t[:, :], kB.t[:, :], op=ALU.add), reads=[kA, kB], writes=[kr2])
                fw.op("vector", lambda h: h.tensor_tensor(kr2.t[:, 64:128], kA.t[:, :], kB.t[:, :], op=ALU.add), reads=[kA, kB], writes=[kr2])
                p2 = nextpx()
                fw.op("tensor", lambda h, p2=p2: h.transpose(p2.t[:, 0:128], kr2.t[:, :], identF.t[:, :]), reads=[kr2, identF], writes=[p2])
                fw.op("scalar", lambda h, p2=p2, s=s: h.copy(kiT.t[:, t0 + s * 128:t0 + (s + 1) * 128], p2.t[:, 0:128]),
                      reads=[p2], writes=[kiTd[g]])

        def P_b(g):
            xt = xg[g % 2]
            t0 = g * G
            wch = next_chunk(0)
            for j in range(4):
                p = nextpx()
                fm_matmul(p, wch, j, rhs_t=xT.t, rhs_tile=xT)
                fw.op("scalar", lambda h, p=p, j=j: h.activation(s4a.t[:, j, :], p.t[:, 0:G], AF.Sigmoid),
                      reads=[p], writes=[s4a])
                yield
            if g > 0:
                fw.op("gpsimd", lambda h: h.tensor_copy(u.t[:, :, 0:30], u.t[:, :, G:G + 30]), reads=[u], writes=[u])
            wch = next_chunk(1)
            for j in range(4):
                p = nextpx()
                fm_matmul(p, wch, j, rhs_t=xT.t, rhs_tile=xT)
                fw.op("vector", lambda h, p=p, j=j: h.tensor_tensor(u.t[:, j, 30:30 + G], p.t[:, 0:G], s4a.t[:, j, :], op=ALU.mult),
                      reads=[p, s4a], writes=[u])
                yield
            for _ in rope_proj(g, 2, qT, lambda j: qT.t[:, j, :]):
                yield
            for _ in rope_proj(g, 4, KTd[g], lambda j: KT.t[:, j, t0:t0 + G]):
                yield
            wch = next_chunk(8)
            for s in range(NSUB):
                p = nextpx()
                for kc in range(8):
                    fw.op("tensor", lambda h, p=p, kc=kc, s=s, wch=wch: h.matmul(
                        p.t[:, 0:512], xT.t[:, kc, s * 128:(s + 1) * 128], wch.t[:, kc * 512:(kc + 1) * 512],
                        start=(kc == 0), stop=(kc == 7)), reads=[xT, wch], writes=[p])
                kt = g * NSUB + s
                fw.op("scalar", lambda h, p=p, kt=kt: h.copy(
                    Vp.t[:, kt, :, 0:64], p.t[:, 0:512].rearrange("p (a c) -> p a c", c=64)), reads=[p], writes=[Vpd[g]])
                yield
            if g + 1 < NG:
                load_tabs(g + 1)

        def CIA(g):

            def conv_taps():
                for j in range(4):
                    wch = next_chunk(36 + j)
                    pc = nextpx()
                    for i in range(31):
                        fw.op("tensor", lambda h, j=j, i=i, pc=pc, wch=wch: h.matmul(
                            pc.t[:, 0:G], wch.t[:, i * 128:(i + 1) * 128], u.t[:, j, i:i + G], start=(i == 0), stop=(i == 30)),
                            reads=[wch, u], writes=[pc])
                    fw.op("scalar", lambda h, j=j, pc=pc: h.activation(acc.t[:, j, :], pc.t[:, 0:G], AF.Identity, bias=cv(124 + j), scale=1.0),
                          reads=[pc, cvec], writes=[acc])
                    yield

            def conv_ln():
                pst = nextpx()
                for j in range(4):
                    fw.op("scalar", lambda h, j=j: h.activation(s4a.t[:, j, :], acc.t[:, j, :], AF.Square), reads=[acc], writes=[s4a])
                for j in range(4):
                    fw.op("tensor", lambda h, j=j: h.matmul(pst.t[:, 0:G], onesM.t[:, :], acc.t[:, j, :], start=(j == 0), stop=(j == 3)),
                          reads=[onesM, acc], writes=[pst])
                for j in range(4):
                    fw.op("tensor", lambda h, j=j: h.matmul(pst.t[:, G:2 * G], onesM.t[:, :], s4a.t[:, j, :], start=(j == 0), stop=(j == 3)),
                          reads=[onesM, s4a], writes=[pst])
                fw.op("scalar", lambda h: h.copy(cmean.t[:, :], pst.t[:, 0:G]), reads=[pst], writes=[cmean])
                fw.op("vector", lambda h: h.tensor_tensor(crstd.t[:, :], cmean.t[:, :], cmean.t[:, :], op=ALU.mult), reads=[cmean], writes=[crstd])
                fw.op("vector", lambda h: h.tensor_tensor(crstd.t[:, :], pst.t[:, G:2 * G], crstd.t[:, :], op=ALU.subtract),
                      reads=[pst, crstd], writes=[crstd])
                fw.op("scalar", lambda h: h.activation(crstd.t[:, :], crstd.t[:, :], AF.Sqrt, bias=cv_eps(), scale=1.0),
                      reads=[crstd, epsT], writes=[crstd])
                fw.op("vector", lambda h: h.reciprocal(crstd.t[:, :], crstd.t[:, :]), reads=[crstd], writes=[crstd])
                for j in range(4):
                    fw.op("gpsimd", lambda h, j=j: h.tensor_tensor(acc.t[:, j, :], acc.t[:, j, :], cmean.t[:, :], op=ALU.subtract),
                          reads=[acc, cmean], writes=[acc])
                    fw.op("gpsimd", lambda h, j=j: h.tensor_tensor(acc.t[:, j, :], acc.t[:, j, :], crstd.t[:, :], op=ALU.mult),
                          reads=[acc, crstd], writes=[acc])
                    fw.op("scalar", lambda h, j=j: h.activation(convact.t[:, j, :], acc.t[:, j, :], AF.Silu, bias=cv(132 + j), scale=cv(128 + j)),
                          reads=[acc, cvec], writes=[convact])

            def indexer_a(s):
                qt = g * NSUB + s
                n = (qt + 1) * 128
                nchunks = (n + 511) // 512
                it = 0
                for c in range(nchunks):
                    wc = min(512, n - 512 * c)
                    kdeps = [kiTd[gg] for gg in range((512 * c) // G, (512 * c + wc - 1) // G + 1)]
                    for hd in range(8):
                        p = nextpx()
                        rl = relu_ts[it % 2]
                        it += 1
                        r0 = 64 * (hd % 2)
                        fw.op("tensor", lambda h, p=p, hd=hd, r0=r0, c=c, wc=wc: h.matmul(
                            p.t[:, 0:wc], qiT.t[r0:r0 + 64, hd // 2, s * 128:(s + 1) * 128],
                            kiT.t[r0:r0 + 64, 512 * c:512 * c + wc], start=True, stop=True),
                            reads=[qiT] + kdeps, writes=[p])
                        fw.op("scalar", lambda h, p=p, wc=wc, rl=rl: h.activation(rl.t[:, 0:wc], p.t[:, 0:wc], AF.Relu),
                              reads=[p], writes=[rl])
                        if hd == 0:
                            fw.op("vector", lambda h, c=c, wc=wc, rl=rl: h.tensor_scalar(
                                iscore.t[:, 512 * c:512 * c + wc], rl.t[:, 0:wc], wsc.t[:, s, 0:1], None, op0=ALU.mult),
                                reads=[rl, wsc], writes=[iscore])
                        else:
                            fw.op("vector", lambda h, c=c, wc=wc, hd=hd, rl=rl: h.scalar_tensor_tensor(
                                iscore.t[:, 512 * c:512 * c + wc], rl.t[:, 0:wc], wsc.t[:, s, hd:hd + 1],
                                iscore.t[:, 512 * c:512 * c + wc], op0=ALU.mult, op1=ALU.add),
                                reads=[rl, wsc, iscore], writes=[iscore])
                        yield
                fw.op("vector", lambda h: h.memset(iscore.t[0:64, n - 64:n], NEG), writes=[iscore])
                yield

            def bisect(s):
                qt = g * NSUB + s
                n = (qt + 1) * 128
                if n - 64 > topk:
                    fw.op("vector", lambda h: h.tensor_reduce(sc(c_mx), iscore.t[:, 0:n], axis=AX.X, op=ALU.max),
                          reads=[iscore], writes=[small])
                    fw.op("vector", lambda h: h.tensor_reduce(sc(c_lo), iscore.t[:, 0:n - 64], axis=AX.X, op=ALU.min),
                          reads=[iscore], writes=[small])
                    fw.op("vector", lambda h: h.tensor_tensor(sc(c_rng), sc(c_mx), sc(c_lo), op=ALU.subtract), reads=[small], writes=[small])
                    fw.op("vector", lambda h: h.tensor_scalar(steps.t[:, :], cvec.t[:, 152:152 + nb], sc(c_rng), None, op0=ALU.mult),
                          reads=[small, cvec], writes=[steps])
                    nd = max(16, int(round(n * 0.46 / 16.0)) * 16)
                    n_act = n - nd
                    thrc = float(2 * topk - 1 - n_act)
                    ht_ap = half_thr(thrc)
                    nb_t = max(16, nb - int(np.floor(np.log2(4096.0 / n)))) if n < 4096 else nb
                    fw.op("vector", lambda h: h.tensor_tensor(t_mid.t[:, :], sc(c_lo), steps.t[:, 0:1], op=ALU.add),
                          reads=[small, steps], writes=[t_mid])
                    for k in range(nb_t):
                        fw.op("vector", lambda h: h.tensor_scalar(junk.t[:, 0:nd], iscore.t[:, 0:nd], t_mid.t[:, 0:1], None,
                                                                  op0=ALU.is_ge, op1=ALU.add, accum_out=t_cd.t[:, 0:1]),
                              reads=[iscore, t_mid], writes=[junk_lo, t_cd])
                        fw.op("scalar", lambda h: h.activation(junk.t[:, nd:n], iscore.t[:, nd:n], AF.Sign, bias=t_mid.t[:, 0:1], scale=-1.0,
                                                               accum_out=t_sg.t[:, 0:1]),
                              reads=[iscore, t_mid], writes=[junk_hi, t_sg])
                        fw.op("scalar", lambda h: h.activation(t_t.t[:, :], t_sg.t[:, :], AF.Identity, bias=ht_ap, scale=0.5),
                              reads=[t_sg, cvec2], writes=[t_t])
                        fw.op("vector", lambda h, k=k: h.scalar_tensor_tensor(t_d.t[:, :], t_cd.t[:, :], t_t.t[:, 0:1], steps.t[:, k:k + 1],
                                                                               op0=ALU.is_ge, op1=ALU.mult),
                              reads=[t_cd, t_t, steps], writes=[t_d])
                        if k < nb_t - 1:
                            fw.op("vector", lambda h, k=k: h.scalar_tensor_tensor(t_mid.t[:, :], t_d.t[:, :], steps.t[:, k + 1:k + 2], t_mid.t[:, :],
                                                                                   op0=ALU.subtract, op1=ALU.add),
                                  reads=[t_d, steps, t_mid], writes=[t_mid])
                        else:
                            fw.op("vector", lambda h, k=k: h.scalar_tensor_tensor(sc(c_lo), t_d.t[:, :], steps.t[:, k:k + 1], t_mid.t[:, :],
                                                                                   op0=ALU.subtract, op1=ALU.add),
                                  reads=[t_d, steps, t_mid], writes=[small])
                        yield
                else:
                    fw.op("vector", lambda h: h.memset(sc(c_lo), -1.0e29), writes=[small])
                    yield

            def indexer_b(s):
                qt = g * NSUB + s
                mT = maskTd[qt % 2]
                mTbase = (qt % 2) * 4096
                nkt = qt + 1
                for kb in range((nkt + 7) // 8):
                    k0 = kb * 8
                    k1 = min(nkt, k0 + 8)
                    w8 = (k1 - k0) * 128
                    fw.op("vector", lambda h, k0=k0, w8=w8: h.tensor_scalar(mask8.t[:, 0:w8], iscore.t[:, k0 * 128:k0 * 128 + w8],
                                                                            sc(c_lo), -30000.0, op0=ALU.is_lt, op1=ALU.mult),
                          reads=[iscore, small], writes=[mask8])
                    for kk in range(k1 - k0):
                        fw.op("tensor", lambda h, kk=kk: h.transpose(pTb.t[:, kk * 128:(kk + 1) * 128], mask8.t[:, kk * 128:(kk + 1) * 128],
                                                                     identB.t[:, :]), reads=[mask8, identB], writes=[pTb])
                    fw.op("scalar", lambda h, k0=k0, w8=w8: h.copy(big16.t[:, mTbase + k0 * 128:mTbase + k0 * 128 + w8], pTb.t[:, 0:w8]),
                          reads=[pTb], writes=[mT])

            def attention(s):
                qt = g * NSUB + s
                mT = maskTd[qt % 2]
                mTbase = (qt % 2) * 4096
                nkt = qt + 1

                def emit_S(kt):
                    gk = kt // NSUB
                    banks = [pX[2 * (kt % 2)], pX[2 * (kt % 2) + 1]]
                    for hd in range(8):
                        r0 = 64 * (hd % 2)
                        bk = banks[hd % 2]
                        fw.op("tensor", lambda h, hd=hd, r0=r0, bk=bk: h.matmul(
                            bk.t[:, (hd // 2) * 128:(hd // 2 + 1) * 128],
                            KT.t[r0:r0 + 64, hd // 2, kt * 128:(kt + 1) * 128],
                            qT.t[r0:r0 + 64, hd // 2, s * 128:(s + 1) * 128], start=(hd // 2 == 0), stop=False,
                            skip_group_check=True), reads=[KTd[gk], qT], writes=[bk])
                    for hb in range(2):
                        bk = banks[hb]
                        fw.op("tensor", lambda h, bk=bk: h.matmul(
                            bk.t[:, 0:512].rearrange("p (a c) -> p a c", c=128), identB.t[:, :],
                            big16.t[:, mTbase + kt * 128:mTbase + (kt + 1) * 128].unsqueeze(1).broadcast_to([128, 4, 128]),
                            start=False, stop=True, skip_group_check=True), reads=[identB, mT], writes=[bk])

                def emit_exp(kt):
                    banks = [pX[2 * (kt % 2)], pX[2 * (kt % 2) + 1]]
                    pts = [PT[(2 * kt) % 4], PT[(2 * kt + 1) % 4]]
                    for hb in range(2):
                        fw.op("scalar", lambda h, hb=hb: h.activation(pts[hb].t[:, :], banks[hb].t[:, :], AF.Exp, scale=0.125),
                              reads=[banks[hb]], writes=[pts[hb]])

                def emit_pv(kt):
                    gk = kt // NSUB
                    pts = [PT[(2 * kt) % 4], PT[(2 * kt + 1) % 4]]
                    for hd in range(8):
                        ob = pO[hd // 4]
                        fw.op("tensor", lambda h, hd=hd, ob=ob: h.matmul(
                            ob.t[:, (hd % 4) * 65:(hd % 4) * 65 + 65],
                            pts[hd % 2].t[:, (hd // 2) * 128:(hd // 2 + 1) * 128],
                            Vp.t[:, kt, hd, :], start=(kt == 0 and hd % 4 == 0), stop=(kt == nkt - 1),
                            skip_group_check=True), reads=[pts[hd % 2], Vpd[gk]], writes=[ob])

                emit_S(0)
                emit_exp(0)
                for kt in range(nkt):
                    if kt + 1 < nkt:
                        emit_S(kt + 1)
                        emit_exp(kt + 1)
                    emit_pv(kt)
                    yield

            def attention_tail(s):
                for b in range(2):
                    fw.op("vector", lambda h, b=b: h.reciprocal(rden.t[:, 4 * b:4 * b + 4], pO[b].t[:, 64:260:65]), reads=[pO[b]], writes=[rden])
                    fw.op("vector", lambda h, b=b: h.tensor_tensor(
                        attn.t[:, 256 * b:256 * (b + 1)].rearrange("p (a c) -> p a c", c=64),
                        pO[b].t[:, 0:260].rearrange("p (a c) -> p a c", c=65)[:, :, 0:64],
                        rden.t[:, 4 * b:4 * b + 4].unsqueeze(2).broadcast_to([128, 4, 64]), op=ALU.mult),
                        reads=[pO[b], rden], writes=[attn])
                for j in range(4):
                    fw.op("tensor", lambda h, j=j: h.transpose(pTb.t[:, j * 128:(j + 1) * 128], attn.t[:, j * 128:(j + 1) * 128], identB.t[:, :]),
                          reads=[attn, identB], writes=[pTb])
                fw.op("scalar", lambda h: h.copy(attnT.t[:, :, s * 128:(s + 1) * 128],
                                                 pTb.t[:, 0:512].rearrange("p (a c) -> p a c", c=128)), reads=[pTb], writes=[attnT])

            return dict(conv_taps=conv_taps, conv_ln=conv_ln, indexer_a=indexer_a, bisect=bisect, indexer_b=indexer_b, attention=attention,
                        attention_tail=attention_tail)

        def M_step(g):
            xt = xg[g % 2]
            for j in range(8):
                wM = next_chunk(10 + j)
                pg = nextpx()
                fm_matmul(pg, wM, 0, rhs_t=xT.t, rhs_tile=xT, col0=0)
                fm_matmul(pg, wM, 1, rhs_t=xT.t, rhs_tile=xT, col0=G)
                po = nextpx()
                fm_matmul(po, wM, 4, rhs_t=convact.t, rhs_tile=convact, nk=4, col0=0)
                fm_matmul(po, wM, 5, rhs_t=attnT.t, rhs_tile=attnT, nk=4, col0=G)
                fw.op("scalar", lambda h, pg=pg, j=j: h.activation(s4a.t[:, 0, :], pg.t[:, 0:G], AF.Sigmoid, bias=cv(136 + j), scale=1.0),
                      reads=[pg, cvec], writes=[s4a])
                fw.op("scalar", lambda h, pg=pg, j=j: h.activation(s4a.t[:, 1, :], pg.t[:, G:2 * G], AF.Sigmoid, bias=cv(144 + j), scale=1.0),
                      reads=[pg, cvec], writes=[s4a])
                fw.op("vector", lambda h, po=po: h.tensor_tensor(s4a.t[:, 2, :], po.t[:, 0:G], s4a.t[:, 0, :], op=ALU.mult),
                      reads=[po, s4a], writes=[s4a])
                fw.op("vector", lambda h, po=po: h.tensor_tensor(s4a.t[:, 3, :], po.t[:, G:2 * G], s4a.t[:, 1, :], op=ALU.mult),
                      reads=[po, s4a], writes=[s4a])
                fw.op("gpsimd", lambda h, j=j: h.tensor_tensor(mergedT.t[:, j, :], s4a.t[:, 2, :], s4a.t[:, 3, :], op=ALU.add),
                      reads=[s4a], writes=[mergedT])
            for hh in range(2):
                wch = next_chunk(18 + hh)
                for s in range(NSUB):
                    p = nextpx()
                    for kc in range(8):
                        fw.op("tensor", lambda h, p=p, kc=kc, s=s, wch=wch: h.matmul(
                            p.t[:, 0:512], mergedT.t[:, kc, s * 128:(s + 1) * 128], wch.t[:, kc * 512:(kc + 1) * 512],
                            start=(kc == 0), stop=(kc == 7)), reads=[mergedT, wch], writes=[p])
                    fw.op("vector", lambda h, p=p, s=s, hh=hh: h.scalar_tensor_tensor(
                        xt.t[:, s, hh * 512:(hh + 1) * 512], xt.t[:, s, hh * 512:(hh + 1) * 512], float(ALPHA), p.t[:, 0:512],
                        op0=ALU.mult, op1=ALU.add), reads=[xt, p], writes=[xt])
            for s in range(NSUB):
                layer_norm_rows(xt, s, 0)
            load_ln(1)

        def F_step(g):
            xt = xg[g % 2]
            t0 = g * G
            transposes_to(xt, x1T_view, s4a)
            yield
            hdeps = maskTd
            for c in range(8):
                wch = next_chunk(20 + c)
                for b in range(4):
                    blk = 4 * c + b
                    p = nextpx()
                    fm_matmul(p, wch, b, rhs_t=x1T_view, rhs_tile=s4a)
                    rt = rtmp if blk % 2 == 0 else rtmp2
                    fw.op("scalar", lambda h, p=p, rt=rt: h.activation(rt.t[:, 0:G], p.t[:, 0:G], AF.Relu), reads=[p], writes=[rt])
                    fw.op("gpsimd", lambda h, blk=blk, rt=rt: h.tensor_tensor(big16.t[:, blk * G:(blk + 1) * G], rt.t[:, 0:G], rt.t[:, 0:G], op=ALU.mult),
                          reads=[rt], writes=hdeps)
                    yield
            for hh in range(2):
                for c in range(4):
                    wch = next_chunk(28 + 4 * hh + c)
                    for s in range(NSUB):
                        for jj in range(8):
                            j = 8 * c + jj
                            fw.op("tensor", lambda h, s=s, j=j, jj=jj, wch=wch: h.matmul(
                                pO[s].t[:, 0:512], big16.t[:, j * G + s * 128:j * G + (s + 1) * 128], wch.t[:, jj * 512:(jj + 1) * 512],
                                start=(j == 0), stop=(j == 31)), reads=hdeps + [wch], writes=[pO[s]])
                        yield
                for s in range(NSUB):
                    fw.op("vector", lambda h, s=s, hh=hh: h.scalar_tensor_tensor(
                        xt.t[:, s, hh * 512:(hh + 1) * 512], xt.t[:, s, hh * 512:(hh + 1) * 512], float(ALPHA), pO[s].t[:, 0:512],
                        op0=ALU.mult, op1=ALU.add), reads=[xt, pO[s]], writes=[xt])
                yield
            for s in range(NSUB):
                layer_norm_rows(xt, s, 1)
                yield
            if g + 1 < NG:
                load_ln(0)
            out_waits.append(fw.dma("gpsimd", lambda: out_d[t0:t0 + G, :].rearrange("(s p) d -> p s d", p=128),
                                    lambda: xt.t[:, :, :], reads=[xt], pe=out_pes[g % 2]))

        def run(gen):
            for _ in gen:
                pass

        def interleave(ga, gb):
            da = db = False
            while not (da and db):
                if not da:
                    try:
                        next(ga)
                    except StopIteration:
                        da = True
                if not db:
                    try:
                        next(gb)
                    except StopIteration:
                        db = True

        def chain(*gens):
            for gg_ in gens:
                for _ in gg_:
                    yield

        def interleave_until(ga, gb, na=1, nb=1):
            while True:
                for _ in range(na):
                    try:
                        next(ga)
                    except StopIteration:
                        return
                for _ in range(nb):
                    try:
                        next(gb)
                    except StopIteration:
                        break

        def CIA_first(g, fns, filler):
            interleave_until(fns["indexer_a"](0), filler, na=3, nb=1)
            interleave_until(fns["bisect"](0), filler, na=1, nb=2)
            run(filler)

        def CIA_rest(g, fns):
            fns["indexer_b"](0)
            filler = chain(fns["attention"](0), fns["conv_taps"]())
            interleave_until(fns["indexer_a"](1), filler, na=4, nb=1)
            interleave_until(fns["bisect"](1), filler, na=1, nb=2)
            run(filler)
            fns["attention_tail"](0)
            fns["conv_ln"]()
            fns["indexer_b"](1)
            run(fns["attention"](1))
            fns["attention_tail"](1)

        assert NSUB == 2
        load_x(0)
        load_tabs(0)
        load_ln(0)
        P_a(0)
        fns = CIA(0)
        CIA_first(0, fns, P_b(0))
        CIA_rest(0, fns)
        for g in range(NG):
            if g + 1 < NG:
                load_x(g + 1)
            M_step(g)
            if g + 1 < NG:
                P_a(g + 1)
                fns = CIA(g + 1)
                CIA_first(g + 1, fns, chain(F_step(g), P_b(g + 1)))
                CIA_rest(g + 1, fns)
            else:
                run(F_step(g))
        fw.finish(out_waits)
    return nc


_PROG_CACHE = {}


def run_cores(xs, consts, wall, L, topk):
    key = (L, topk)
    if key not in _PROG_CACHE:
        _PROG_CACHE[key] = build_program(L, topk)
    nc = _PROG_CACHE[key]
    in_maps = []
    for xb in xs:
        m = {"x": np.ascontiguousarray(xb, dtype=np.float32), "wall": wall}
        m.update(consts)
        in_maps.append(m)
    res = run_bass_kernel_spmd(nc, in_maps, core_ids=list(range(len(xs))))
    return [r["out"] for r in res.results]


def kernel(**inputs):
    x = np.asarray(inputs["x"], np.float32)
    B, L, _ = x.shape
    topk = min(256, L // 4)
    wall = host_weights(inputs)
    consts = host_consts(inputs, L)
    outs = run_cores([x[b] for b in range(B)], consts, wall, L, topk)
    return np.stack(outs, axis=0).astype(np.float32)
```
